# Optimizing a Trainium2 kernel written in Bass

```python
import jax
import jax.numpy as jnp
from jax import lax
import numpy as np

D_MODEL = 1024
BATCH = 2
SEQ = 8192
DEPTH = 1

GRID_W = 64
CTX_LEN = 256
EPS = 1e-6

A_WIDTH = 512
A_GROUPS = 4
A_GROUP_DIM = A_WIDTH // A_GROUPS
A_CHUNK = 128
A_ROW_GROUPS = 2

B_HEADS = 4
B_DK = 64
B_DV = 128
B_KEY_WIDTH = B_HEADS * B_DK
B_VAL_WIDTH = B_HEADS * B_DV
B_GATE_RANK = 16
B_GATE_TAU = 16.0
B_CHUNK = 64

Q0 = 0
K0 = Q0 + B_KEY_WIDTH
V0 = K0 + B_KEY_WIDTH
LR0 = V0 + B_VAL_WIDTH
ZB0 = LR0 + 2 * B_GATE_RANK
UA0 = ZB0 + B_VAL_WIDTH
VA0 = UA0 + A_WIDTH
ZA0 = VA0 + A_WIDTH
G0 = ZA0 + A_WIDTH
IN_WIDTH = G0 + 2 * D_MODEL

kernel_name = "hybrid_gmlp_gla_prefix_block"


def rmsnorm(x, g):
    xf = x.astype(jnp.float32)
    y = xf * lax.rsqrt(jnp.mean(xf * xf, axis=-1, keepdims=True) + EPS)
    return (y * g.astype(jnp.float32)).astype(x.dtype)


def layernorm(x, g, b):
    xf = x.astype(jnp.float32)
    xc = xf - jnp.mean(xf, axis=-1, keepdims=True)
    y = xc * lax.rsqrt(jnp.mean(xc * xc, axis=-1, keepdims=True) + EPS)
    return (y * g.astype(jnp.float32) + b.astype(jnp.float32)).astype(x.dtype)


def spatial_mix(vg, ws, bs):
    b, l, g, dg = vg.shape
    vr = vg.reshape(b, l // A_CHUNK, A_CHUNK, g, dg)
    s = jnp.einsum('gij,bnjgc->bnigc', ws, vr) + bs.T[None, None, :, :, None]
    return s.reshape(b, l, g, dg)


def to_colmajor(t, rows):
    b, l, g, dg = t.shape
    return t.reshape(b, rows, GRID_W, g, dg).transpose(0, 2, 1, 3, 4).reshape(b, l, g, dg)


def from_colmajor(t, rows):
    b, l, g, dg = t.shape
    return t.reshape(b, GRID_W, rows, g, dg).transpose(0, 2, 1, 3, 4).reshape(b, l, g, dg)


def chunk_mlp_branch(p, rows, ln_g, ln_b, ws, bs, w_proj):
    b, l, _ = p.shape
    u = p[..., UA0:VA0]
    z = p[..., ZA0:G0]
    vn = layernorm(p[..., VA0:ZA0], ln_g, ln_b).reshape(b, l, A_GROUPS, A_GROUP_DIM)
    if rows is None:
        sv = spatial_mix(vn, ws, bs)
    else:
        r = A_ROW_GROUPS
        sv_row = spatial_mix(vn[:, :, :r], ws[:r], bs[:r])
        sv_col = from_colmajor(spatial_mix(to_colmajor(vn[:, :, r:], rows), ws[r:], bs[r:]), rows)
        sv = jnp.concatenate([sv_row, sv_col], axis=2)
    return (u * sv.reshape(b, l, A_WIDTH) * jax.nn.silu(z)) @ w_proj


def to_chunks(t):
    b, l, h, d = t.shape
    return t.reshape(b, l // B_CHUNK, B_CHUNK, h, d).transpose(0, 3, 1, 2, 4)


def from_chunks(t):
    b, h, n, c, d = t.shape
    return t.transpose(0, 2, 3, 1, 4).reshape(b, n * c, h, d)


def gla_states(kc, vc, ac, s0):
    cum = jnp.cumsum(ac, axis=3)
    cum_last = cum[:, :, :, -1]
    k_dec = kc * jnp.exp(cum_last[:, :, :, None] - cum)
    kv = jnp.einsum('bhncd,bhnce->bhnde', k_dec, vc)

    def step(s, inp):
        decay, upd = inp
        return decay[..., None] * s + upd, s

    s_final, s_before = lax.scan(step, s0, (jnp.moveaxis(jnp.exp(cum_last), 2, 0), jnp.moveaxis(kv, 2, 0)))
    return s_final, jnp.moveaxis(s_before, 0, 2), cum


def gla_chunk_outputs(qc, kc, vc, cum, s_before):
    qd = qc * jnp.exp(cum)
    kd = kc * jnp.exp(-cum)
    scores = jnp.einsum('bhnid,bhnjd->bhnij', qd, kd)
    mask = jnp.tril(jnp.ones((B_CHUNK, B_CHUNK), dtype=bool))
    scores = jnp.where(mask, scores, 0.0)
    return jnp.einsum('bhnij,bhnje->bhnie', scores, vc) + jnp.einsum('bhnid,bhnde->bhnie', qd, s_before)


def gla_direction(q, k, v, log_a, s0, reverse):
    if reverse:
        k, v, log_a = jnp.flip(k, 1), jnp.flip(v, 1), jnp.flip(log_a, 1)
        if q is not None:
            q = jnp.flip(q, 1)
    kc, vc, ac = to_chunks(k), to_chunks(v), to_chunks(log_a)
    s_final, s_before, cum = gla_states(kc, vc, ac, s0)
    if q is None:
        return None, s_final
    o = from_chunks(gla_chunk_outputs(to_chunks(q), kc, vc, cum, s_before))
    if reverse:
        o = jnp.flip(o, 1)
    return o, s_final


def gla_q(p):
    b, l, _ = p.shape
    return p[..., Q0:K0].reshape(b, l, B_HEADS, B_DK).astype(jnp.float32) * (B_DK ** -0.5)


def gla_kva(p, base, w2, gb):
    b, l, _ = p.shape
    k = p[..., K0 - base:V0 - base].reshape(b, l, B_HEADS, B_DK).astype(jnp.float32)
    v = p[..., V0 - base:LR0 - base].reshape(b, l, B_HEADS, B_DV).astype(jnp.float32)
    lr = p[..., LR0 - base:ZB0 - base].reshape(b, l, 2, B_GATE_RANK)
    logits = jnp.einsum('blrk,rkd->blrd', lr, w2) + gb
    log_a = (jax.nn.log_sigmoid(logits.astype(jnp.float32)) / B_GATE_TAU).reshape(b, l, 2, B_HEADS, B_DK)
    return k, v, log_a[:, :, 0], log_a[:, :, 1]


def gla_branch_out(o, z, g, w_proj):
    b, l, _, _ = o.shape
    on = o * lax.rsqrt(jnp.mean(o * o, axis=-1, keepdims=True) + EPS) * g.reshape(B_HEADS, B_DV).astype(jnp.float32)
    on = on.reshape(b, l, B_VAL_WIDTH).astype(z.dtype)
    return (on * jax.nn.silu(z)) @ w_proj


def merge_branches(p, ya, yb, w_out):
    g = jax.nn.sigmoid(p[..., G0:])
    return (g[..., :D_MODEL] * ya + g[..., D_MODEL:] * yb) @ w_out


def setup_inputs(seed: int = 0) -> dict:
    key = jax.random.key(seed)
    ks = jax.random.split(key, 20)

    def nrm(k, shape, s):
        return jax.random.normal(k, shape, jnp.float32) * s

    return {
        "x": nrm(ks[0], (BATCH, SEQ, D_MODEL), 1.0),
        "c": nrm(ks[1], (BATCH, D_MODEL), 1.0),
        "ctx": nrm(ks[2], (BATCH, CTX_LEN, D_MODEL), 1.0),
        "c_ctx": nrm(ks[3], (D_MODEL,), 1.0),
        "w_mod": nrm(ks[4], (DEPTH, D_MODEL, 3 * D_MODEL), D_MODEL ** -0.5),
        "b_mod": nrm(ks[5], (DEPTH, 3 * D_MODEL), 0.01),
        "norm_g": 1.0 + nrm(ks[6], (DEPTH, D_MODEL), 0.01),
        "w_in": nrm(ks[7], (DEPTH, D_MODEL, IN_WIDTH), D_MODEL ** -0.5),
        "a_ln_g": 1.0 + nrm(ks[8], (DEPTH, A_WIDTH), 0.01),
        "a_ln_b": nrm(ks[9], (DEPTH, A_WIDTH), 0.01),
        "a_ws": nrm(ks[10], (DEPTH, A_GROUPS, A_CHUNK, A_CHUNK), A_CHUNK ** -0.5),
        "a_bs": 1.0 + nrm(ks[11], (DEPTH, A_GROUPS, A_CHUNK), 0.01),
        "b_gate_w2": nrm(ks[12], (DEPTH, 2, B_GATE_RANK, B_KEY_WIDTH), B_GATE_RANK ** -0.5),
        "b_gate_b": nrm(ks[13], (DEPTH, 2, B_KEY_WIDTH), 0.1),
        "b_norm_g": 1.0 + nrm(ks[14], (DEPTH, B_VAL_WIDTH), 0.01),
        "w_proj_a": nrm(ks[15], (DEPTH, A_WIDTH, D_MODEL), A_WIDTH ** -0.5),
        "w_proj_b": nrm(ks[16], (DEPTH, B_VAL_WIDTH, D_MODEL), B_VAL_WIDTH ** -0.5),
        "w_out": nrm(ks[17], (DEPTH, D_MODEL, D_MODEL), D_MODEL ** -0.5),
        "final_norm_g": 1.0 + nrm(ks[18], (D_MODEL,), 0.01),
    }


def reference(x, c, ctx, c_ctx, w_mod, b_mod, norm_g, w_in, a_ln_g, a_ln_b, a_ws, a_bs,
              b_gate_w2, b_gate_b, b_norm_g, w_proj_a, w_proj_b, w_out, final_norm_g):
    rows = x.shape[1] // GRID_W
    xc = ctx
    for layer in range(DEPTH):
        last = layer == DEPTH - 1
        wm, bm, wi = w_mod[layer], b_mod[layer], w_in[layer]

        mod = jax.nn.silu(c) @ wm + bm
        shift, scale, gate = jnp.split(mod, 3, axis=-1)
        h = rmsnorm(x, norm_g[layer]) * (1 + scale[:, None]) + shift[:, None]
        p = h @ wi

        n_mod = (2 if last else 3) * D_MODEL
        mod_c = jax.nn.silu(c_ctx) @ wm[:, :n_mod] + bm[:n_mod]
        hc = rmsnorm(xc, norm_g[layer]) * (1 + mod_c[D_MODEL:2 * D_MODEL]) + mod_c[:D_MODEL]
        base_c = K0 if last else 0
        pc = hc @ (wi[:, K0:ZB0] if last else wi)

        kc, vc, ac_f, ac_b = gla_kva(pc, base_c, b_gate_w2[layer], b_gate_b[layer])
        qc = None if last else gla_q(pc)
        s0 = jnp.zeros((xc.shape[0], B_HEADS, B_DK, B_DV), jnp.float32)
        oc_f, sc_f = gla_direction(qc, kc, vc, ac_f, s0, False)
        oc_b, sc_b = gla_direction(qc, kc, vc, ac_b, s0, True)

        k, v, a_f, a_b = gla_kva(p, 0, b_gate_w2[layer], b_gate_b[layer])
        q = gla_q(p)
        o_f, _ = gla_direction(q, k, v, a_f, sc_f, False)
        o_b, _ = gla_direction(q, k, v, a_b, sc_b, True)
        yb = gla_branch_out(o_f + o_b, p[..., ZB0:UA0], b_norm_g[layer], w_proj_b[layer])

        ya = chunk_mlp_branch(p, rows, a_ln_g[layer], a_ln_b[layer], a_ws[layer], a_bs[layer], w_proj_a[layer])

        x = x + gate[:, None] * merge_branches(p, ya, yb, w_out[layer])

        if not last:
            ybc = gla_branch_out(oc_f + oc_b, pc[..., ZB0:UA0], b_norm_g[layer], w_proj_b[layer])
            yac = chunk_mlp_branch(pc, None, a_ln_g[layer], a_ln_b[layer], a_ws[layer], a_bs[layer], w_proj_a[layer])
            xc = xc + mod_c[2 * D_MODEL:] * merge_branches(pc, yac, ybc, w_out[layer])
    return rmsnorm(x, final_norm_g)
```

```python
import numpy as np
from contextlib import ExitStack
import concourse.bass as bass
import concourse.mybir as mybir
from concourse.bass_utils import run_bass_kernel_spmd

F32 = mybir.dt.float32
BF16 = mybir.dt.bfloat16
ALU = mybir.AluOpType
AF = mybir.ActivationFunctionType

EPS = 1e-6
NT = 16
NTC = 18
LR0, ZB0, UA0, VA0, ZA0, G0 = 1024, 1056, 1568, 2080, 2592, 3104
DEBUG = False
_P0CUT = 99
N_SILU_PASSES = 1
SAME_ENGINE_SYNC = True
_NOCC = False
_P2CUT = 99


class Prog:
    ENG = ('pe', 'act', 'dve', 'pool', 'sp')

    def __init__(self, nc):
        self.nc = nc
        self.ops = {e: [] for e in self.ENG}
        self.cnt = {e: 0 for e in self.ENG}
        self.dcnt = {}
        self.waited = {e: {} for e in self.ENG}
        self.lastw = {}
        self.readers = {}
        self.sem_h = {}
        self.es = ExitStack()

    def sem(self, k):
        if k not in self.sem_h:
            self.sem_h[k] = self.es.enter_context(self.nc.semaphore("s_" + k))
        return self.sem_h[k]

    def _deps(self, reads, writes):
        deps = {}

        def add(p):
            if p is None:
                return
            k, v = p
            if deps.get(k, 0) < v:
                deps[k] = v
        for k in reads:
            add(self.lastw.get(k))
        for k in writes:
            add(self.lastw.get(k))
            for r in self.readers.get(k, ()):
                add(r)
        return deps

    def _waits(self, eng, deps):
        out = []
        for k, v in deps.items():
            if self.waited[eng].get(k, 0) >= v:
                continue
            self.waited[eng][k] = v
            out.append((k, v))
        return out

    def _record(self, pos, reads, writes):
        for k in writes:
            self.lastw[k] = pos
            self.readers[k] = []
        for k in reads:
            if k not in writes:
                self.readers.setdefault(k, []).append(pos)

    def op(self, eng, fn, reads=(), writes=()):
        deps = self._deps(reads, writes)
        if eng == 'pe' or not SAME_ENGINE_SYNC:
            deps.pop(eng, None)
        waits = self._waits(eng, deps)
        self.cnt[eng] += 1
        self.sem(eng)
        self.ops[eng].append((waits, fn, (eng, 1)))
        self._record((eng, self.cnt[eng]), reads, writes)

    def dma(self, queue, fn, reads, writes, key, inc=16):
        deps = self._deps(reads, writes)
        waits = self._waits(queue, deps)
        self.dcnt[key] = self.dcnt.get(key, 0) + inc
        self.sem(key)
        self.ops[queue].append((waits, fn, (key, inc)))
        self._record((key, self.dcnt[key]), reads, writes)

    def dma_group(self, queue, items, key):
        allw = []
        for fn, reads, writes in items:
            deps = self._deps(reads, writes)
            waits = self._waits(queue, deps)
            self.dcnt[key] = self.dcnt.get(key, 0) + 16
            self.sem(key)
            self.ops[queue].append((waits, fn, (key, 16)))
            allw.append((reads, writes))
        pos = (key, self.dcnt[key])
        for reads, writes in allw:
            self._record(pos, reads, writes)

    def barrier(self, exclude=()):
        allsems = {e: self.cnt[e] for e in ('pe', 'act', 'dve', 'pool')}
        allsems.update(self.dcnt)
        for e in self.ENG:
            deps = {k: v for k, v in allsems.items() if v > 0 and k != e and k not in exclude}
            waits = self._waits(e, deps)
            if waits:
                self.ops[e].append((waits, None, None))

    def run(self):
        nc = self.nc
        for k in list(self.dcnt) + list(self.ENG):
            self.sem(k)
        with nc.Block() as block:
            def mk(e):
                def body(eng):
                    for waits, fn, sig in self.ops[e]:
                        for k, v in waits:
                            eng.wait_ge(self.sem(k), v)
                        if fn is None:
                            continue
                        ins = fn(eng)
                        ins.then_inc(self.sem(sig[0]), sig[1])
                return body
            block.tensor(mk('pe'))
            block.scalar(mk('act'))
            block.vector(mk('dve'))
            block.gpsimd(mk('pool'))
            block.sync(mk('sp'))
        for e in self.ENG:
            self.ops[e] = []


def build_nc(stop=None, dump_hook=None):
    nc = bass.Bass("TRN2", target_bir_lowering=False)
    P = Prog(nc)

    def end_phase(k, env, exclude=()):
        P.barrier(exclude)
        if dump_hook is not None:
            for name, ap in dump_hook(k, env):
                shp = list(ap.shape)
                dt_ = nc.dram_tensor("dbg_" + name, shp, ap.dtype, kind="ExternalOutput").ap()
                P.dma('sp', lambda e, dt_=dt_, ap=ap: e.dma_start(out=dt_, in_=ap), [], [], 'dbg')
            P.barrier()
        P.run()
        return stop == k

    def din(name, shape, dt=F32):
        return nc.dram_tensor(name, list(shape), dt, kind="ExternalInput").ap()

    x_d = din("x", [2048, 1024])
    ctx_d = din("ctx", [256, 1024])
    cT_d = din("cT", [128, 8])
    cctxT_d = din("cctxT", [128, 8])
    wmod_d = din("wmod", [1024, 3072])
    bmod_d = din("bmod", [1, 3072])
    normgT_d = din("normgT", [128, 8])
    win_d = din("win", [1024, 5152])
    alng_d = din("alng", [128, 512])
    alnb_d = din("alnb", [128, 512])
    wsT_d = din("wsT", [2, 128, 128])
    wsTcol_d = din("wsTcol", [2, 128, 32])
    bsrow_d = din("bsrow", [128, 2, 128])
    bscol_d = din("bscol", [128, 2, 32])
    w2aug_d = din("w2aug", [33, 512])
    bnormgT_d = din("bnormgT", [128, 4])
    wpa_d = din("wpa", [512, 1024])
    wpb_d = din("wpb", [512, 1024])
    wout_d = din("wout", [1024, 1024])
    fng_d = din("fng", [128, 1024])
    consts_d = din("consts", [7, 128, 128])
    fsel_d = din("fsel", [128, 16])
    normgbc_d = din("normgbc", [128, 1024])
    bsrows_d = din("bsrows", [1, 4, 512])
    out_d = nc.dram_tensor("out", [2048, 1024], F32, kind="ExternalOutput").ap()

    vn_in = nc.dram_tensor("vn_in", [2048, 256], BF16)
    vn_all = nc.dram_tensor("vn_all", [8192, 256], BF16)
    st_in = nc.dram_tensor("st_in", [128, 528], F32)
    st_all = nc.dram_tensor("st_all", [512, 528], F32)

    win_v = win_d.rearrange("(kc p) n -> p kc n", p=128)
    wmod_v = wmod_d.rearrange("(kc p) n -> p kc n", p=128)

    esA = ExitStack()
    _uid = []

    def alloc(es, name, shape, dt, side="left"):
        return es.enter_context(nc.sbuf_tensor("sb_" + name + "_%d" % len(_uid), list(shape), dt, side=side)) if not _uid.append(0) else None

    hT = alloc(esA, "hT", [128, 8, 2048], BF16)
    vn = alloc(esA, "vn", [128, NT, 512], BF16)
    wch = alloc(esA, "wch", [128, 4, 8, 256], BF16)
    identB = alloc(esA, "identB", [128, 128], BF16)
    onesB = alloc(esA, "onesB", [128, 128], BF16)
    M4 = alloc(esA, "M4", [128, 2, 4, 128], BF16)
    gate_bc = alloc(esA, "gate_bc", [128, 1024], F32)
    dec = alloc(esA, "dec", [128, 2, 2, NTC], F32)
    Pn = alloc(esA, "Pn", [128, 2, 2, NT + 1], F32)
    stat = alloc(esA, "stat", [128, 2, NTC], F32)
    w2aug = alloc(esA, "w2aug", [33, 512], BF16)
    wsT = alloc(esA, "wsT", [128, 2, 128], BF16)
    wsTcol = alloc(esA, "wsTcol", [128, 2, 32], BF16)
    bnormgT = alloc(esA, "bnormgT", [128, 4], F32)
    gs = alloc(esA, "gs", [128, 4], F32)
    fsel = alloc(esA, "fsel", [128, 16], F32)

    ps = [esA.enter_context(nc.psum_tensor("ps%d" % i, [128, 512], F32)) for i in range(8)]
    bank_ctr = [0]

    def bank():
        b = bank_ctr[0] % 8
        bank_ctr[0] += 1
        return b

    def st_col(kind, t):
        return kind * 18 + t

    def pe_group(mms, reads, writes):
        def fn(pe):
            ins = None
            for m in mms:
                if m[0] == 'T':
                    ins = pe.transpose(out=m[1], in_=m[2], identity=m[3])
                else:
                    ins = pe.matmul(m[0], lhsT=m[1], rhs=m[2], start=m[3], stop=m[4])
            return ins
        P.op('pe', fn, reads, writes)

    def act(out, in_, func, reads, writes, **kw):
        P.op('act', lambda e: e.activation(out=out, in_=in_, func=func, **kw), reads, writes)

    def dve(fname, reads, writes, **kw):
        P.op('dve', lambda e: getattr(e, fname)(**kw), reads, writes)

    def pool(fname, reads, writes, **kw):
        P.op('pool', lambda e: getattr(e, fname)(**kw), reads, writes)

    def dma(queue, out, in_, reads, writes, key):
        P.dma(queue, lambda e: e.dma_start(out=out, in_=in_), reads, writes, key)

    def load_w(slot, col0, ncols, dstcol=0):
        dma('pool', wch[:, slot, :, dstcol:dstcol + ncols], win_v[:, :, col0:col0 + ncols],
            [], ['w%d' % slot], 'wl%d' % slot)

    def rstd_from(out, in_, scale, bias, key_in, key_out):
        act(out, in_, AF.Ln, [key_in], [key_out], bias=bias, scale=scale)
        act(out, out, AF.Exp, [key_out], [key_out], scale=-0.5)

    dbg = []

    esB = ExitStack()
    qd = alloc(esB, "qd", [128, 2, 2, 2048], BF16)
    kd = alloc(esB, "kd", [128, 2, 2, 2048], BF16)
    vtok = alloc(esB, "vtok", [128, NTC, 512], BF16)
    Sloc = alloc(esB, "Sloc", [128, 2, 2, NT, 128], BF16)
    Sst = alloc(esB, "Sst", [128, 2, 2, 2, 128], F32)
    Sctx = alloc(esB, "Sctx", [128, 2, 2, 128], F32)
    stg = alloc(esB, "stg", [128, 2, 2, 132], F32)

    bcf = qd[:].rearrange("p a b t -> p (a b t)").bitcast(F32)
    Abc = [bcf[:, 0:1024], bcf[:, 2048:3072]]
    Bbc = [bcf[:, 1024:2048], bcf[:, 3072:4096]]
    kdf = kd[:].rearrange("p a b t -> p (a b t)")
    Bbf = [kdf[:, 0:1024], kdf[:, 1024:2048]]

    esX = ExitStack()
    xt = alloc(esX, "xt", [128, 2, 1024], F32, side="right")
    xn = alloc(esX, "xn", [128, 2, 1024], BF16, side="right")

    def a_load(t):
        s = t % 2
        src = x_d[t * 128:(t + 1) * 128, :] if t < NT else ctx_d[(t - NT) * 128:(t - NT + 1) * 128, :]
        dma('sp', xt[:, s], src, [], ['xt%d' % s], 'x%d' % s)
        ssc = stat[:, 0, t:t + 1]
        rsc = stat[:, 1, t:t + 1]
        act(xn[:, s], xt[:, s], AF.Square, ['xt%d' % s], ['xn%d' % s, 'stat%d' % t], accum_out=ssc)
        rstd_from(rsc, ssc, 1.0 / 1024, EPS, 'stat%d' % t, 'stat%d' % t)

    esR = ExitStack()
    wmc = alloc(esR, "wmc", [128, 2, 8, 512], BF16, side="right")
    bmr = alloc(esR, "bmr", [1, 2, 512], F32, side="right")
    modrep = alloc(esR, "modrep", [128, 2048], F32, side="right")
    stat_mix = alloc(esR, "stat_mix", [128, 2, 8, 128], BF16, side="right")
    stat_c = alloc(esR, "stat_c", [128, 2, 8, 128], BF16, side="right")
    schl = alloc(esR, "schl", [128, 2, 2, 8], F32, side="right")
    sc16 = alloc(esR, "sc16", [128, 2, 8], BF16, side="right")
    cTs = alloc(esR, "cTs", [128, 8], F32, side="right")
    cctxTs = alloc(esR, "cctxTs", [128, 8], F32, side="right")
    sc = alloc(esR, "sc", [128, 8], F32, side="right")
    identF = alloc(esR, "identF", [128, 128], F32, side="right")
    onesF = alloc(esR, "onesF", [128, 128], F32, side="right")
    normgbc = alloc(esR, "normgbc", [128, 1024], F32, side="right")
    scc = alloc(esR, "scc", [128, 8], F32, side="right")

    def phase0_body():
        items = [
            (lambda e: e.dma_start(out=identF[:], in_=consts_d[0, :, :]), [], ['identF']),
            (lambda e: e.dma_start(out=cTs[:], in_=cT_d[:, :]), [], ['cTs']),
            (lambda e: e.dma_start(out=cctxTs[:], in_=cctxT_d[:, :]), [], ['cctxTs']),
            (lambda e: e.dma_start(out=bnormgT[:], in_=bnormgT_d[:, :]), [], ['bnormgT']),
            (lambda e: e.dma_start(out=fsel[:], in_=fsel_d[:, :]), [], ['fsel']),
            (lambda e: e.dma_start(out=normgbc[:], in_=normgbc_d[:, :]), [], ['normgbc']),
        ]
        P.dma_group('sp', items[1:3], 'setup0')
        for ch in range(2):
            dma('pool', wmc[:, ch], wmod_v[:, :, ch * 512:(ch + 1) * 512], [], ['wmc%d' % ch], 'wm%d' % ch)
        P.dma_group('sp', items[:1] + items[3:], 'setup')
        citems = [
            (lambda e: e.dma_start(out=identB[:], in_=consts_d[0, :, :]), [], ['identB']),
            (lambda e: e.dma_start(out=w2aug[:], in_=w2aug_d[:, :]), [], ['w2aug']),
            (lambda e: e.dma_start(out=wsT[:], in_=wsT_d.rearrange("g p c -> p g c")), [], ['wsT']),
            (lambda e: e.dma_start(out=wsTcol[:], in_=wsTcol_d.rearrange("g p c -> p g c")), [], ['wsTcol']),
        ]
        for d in range(2):
            for h in range(4):
                citems.append((lambda e, d=d, h=h: e.dma_start(out=M4[:, d, h, :], in_=consts_d[5 + d, :, :]),
                               [], ['M4']))
        P.dma_group('pool', citems, 'setupc')

        if _P0CUT < 2:
            return
        dve('memset', [], ['onesB'], ap=onesB[:], constant=1.0)
        dve('memset', [], ['onesF'], ap=onesF[:], constant=1.0)
        dve('memset', [], ['Pn'], ap=Pn[:], constant=1.0)
        dve('memset', [], ['stat%d' % t for t in range(NTC)], ap=stat[:], constant=0.0)

        act(sc[:], cTs[:], AF.Silu, ['cTs'], ['sc'])
        act(scc[:], cctxTs[:], AF.Silu, ['cctxTs'], ['scc'])
        for j, src in enumerate((sc, scc)):
            dve('tensor_copy', ['sc', 'scc'], ['sc16'], out=sc16[:, j, :], in_=src[:])
            dve('tensor_copy', ['sc16'], ['schl'], out=schl[:, j, 0, :], in_=sc16[:, j, :])
            if N_SILU_PASSES > 1:
                dve('tensor_tensor', ['sc', 'scc', 'schl'], ['schl'], out=schl[:, j, 1, :], in0=src[:], in1=schl[:, j, 0, :],
                    op=ALU.subtract)
        for hl in range(N_SILU_PASSES):
            dve('tensor_copy', ['schl'], ['stat_mix'], out=stat_mix[:, hl, :, 0:64],
                in_=schl[:, 0, hl, :].unsqueeze(2).to_broadcast([128, 8, 64]))
            dve('tensor_copy', ['schl'], ['stat_mix'], out=stat_mix[:, hl, :, 64:128],
                in_=schl[:, 1, hl, :].unsqueeze(2).to_broadcast([128, 8, 64]))
            dve('tensor_copy', ['schl'], ['stat_c'], out=stat_c[:, hl, :, :],
                in_=schl[:, 0, hl, :].unsqueeze(2).to_broadcast([128, 8, 128]))

        if _P0CUT < 3:
            return
        for ch in range(6):
            slot = ch % 2
            c0 = ch * 512
            if ch >= 2:
                dma('pool', wmc[:, slot], wmod_v[:, :, c0:c0 + 512], [], ['wmc%d' % slot], 'wm%d' % slot)
            b = bank()
            st_t = stat_mix if ch < 4 else stat_c
            mms = []
            for hl in range(N_SILU_PASSES):
                for kc in range(8):
                    mms.append((ps[b][:], st_t[:, hl, kc, :], wmc[:, slot, kc, :], hl == 0 and kc == 0, False))
            dma('sp', bmr[0:1, slot, :], bmod_d[0:1, c0:c0 + 512], [], ['bmr%d' % slot], 'bm%d' % slot)
            mms.append((ps[b][:], onesF[0:1, :], bmr[0:1, slot, :], False, True))
            pe_group(mms, ['stat_mix', 'stat_c', 'wmc%d' % slot, 'bmr%d' % slot, 'onesF'], ['ps%d' % b])
            if ch < 4:
                act(modrep[:, c0:c0 + 512], ps[b][:], AF.Copy, ['ps%d' % b], ['modrep'])
            else:
                act(gate_bc[:, c0 - 2048:c0 - 2048 + 512], ps[b][:], AF.Copy, ['ps%d' % b], ['gate_bc'])
        load_w(0, VA0, 256)
        load_w(1, VA0 + 256, 256)
        load_w(2, 512, 256)
        load_w(3, 768, 256)
        if _P0CUT < 4:
            return
        for l, row in ((0, 0), (1, 64)):
            for n in range(4):
                b = bank()
                pe_group([(ps[b][:], onesF[row:row + 1, :], modrep[row:row + 1, n * 512:(n + 1) * 512], True, True)],
                         ['modrep', 'onesF'], ['ps%d' % b])
                if n < 2:
                    cs = slice(n * 512, (n + 1) * 512)
                    act(Bbf[l][:, cs], ps[b][:], AF.Copy, ['ps%d' % b], ['bc'])
                else:
                    dve('scalar_tensor_tensor', ['ps%d' % b, 'normgbc'], ['bc'],
                        out=Abc[l][:, (n - 2) * 512:(n - 1) * 512], in0=ps[b][:], scalar=1.0,
                        in1=normgbc[:, (n - 2) * 512:(n - 1) * 512], op0=ALU.add, op1=ALU.mult)
        dve('tensor_scalar_mul', ['bnormgT'], ['gs'], out=gs[:], in0=bnormgT[:], scalar1=float(np.sqrt(128.0)))
        a_load(0)
        a_load(1)

    phase0_body()
    if end_phase(0, locals(), exclude=('wl0', 'wl1', 'wl2', 'wl3') if stop is None else ()):
        esR.close(); esX.close(); esB.close(); esA.close(); P.es.close()
        return nc
    esR.close()

    esC = ExitStack()
    R = "right"
    hcT = alloc(esC, "hcT", [128, 8, 256], BF16, R)
    kdec_b = alloc(esC, "kdec_b", [128, NTC, 256], BF16, R)
    kdec_f = alloc(esC, "kdec_f", [128, 2, 256], BF16, R)
    alng = alloc(esC, "alng", [128, 512], F32, R)
    alnb = alloc(esC, "alnb", [128, 512], F32, R)
    vraw = alloc(esC, "vraw", [128, 2, 512], F32, R)
    Lf = alloc(esC, "Lf", [128, 512], F32, R)
    Lhl = alloc(esC, "Lhl", [128, 2, 2, 512], BF16, R)
    Ltmp = alloc(esC, "Ltmp", [128, 512], F32, R)
    Ep = alloc(esC, "Ep", [128, 2, 2, 512], BF16, R)
    Em = alloc(esC, "Em", [128, 2, 2, 512], BF16, R)
    Xb = alloc(esC, "Xb", [128, 2, 512], BF16, R)
    lrT1 = alloc(esC, "lrT1", [33, 512], BF16, R)
    lnst = alloc(esC, "lnst", [128, 8, NT], F32, R)
    wlr = alloc(esC, "wlr", [128, 8, 32], BF16, R)
    cst = alloc(esC, "cst", [128, 4, 128], BF16, R)

    P.dma_group('sp', [
        (lambda e: e.dma_start(out=alng[:], in_=alng_d[:, :]), [], ['alng']),
        (lambda e: e.dma_start(out=alnb[:], in_=alnb_d[:, :]), [], ['alnb']),
    ], 'setup')
    P.dma('pool', lambda e: e.dma_start(out=cst[:], in_=consts_d[1:5, :, :].rearrange("k p c -> p k c")), [], ['cst'],
          'cstl')
    dve('memset', [], ['Sst00', 'Sst01', 'Sst10', 'Sst11'], ap=Sst[:], constant=0.0)
    dve('memset', [], ['Sctx0', 'Sctx1'], ap=Sctx[:], constant=0.0)
    dve('memset', [], ['stg'], ap=stg[:], constant=0.0)
    dve('memset', [], ['lnst%d' % t for t in range(NT)], ap=lnst[:], constant=0.0)
    dve('memset', [], ['lrT1'], ap=lrT1[32:33, :], constant=1.0)

    dma('pool', wlr[:], win_v[:, :, LR0:LR0 + 32], [], ['wlr'], 'wlrl')

    def hkeys(tiles):
        return ['hT%d' % t for t in tiles]

    def hT_tile(t):
        if t < NT:
            return lambda kc: hT[:, kc, t * 128:(t + 1) * 128]
        return lambda kc: hcT[:, kc, (t - NT) * 128:(t - NT + 1) * 128]

    pbank = {}

    def a_mod(t):
        s = t % 2
        l = 0 if t < NT else 1
        rsc = stat[:, 1, t:t + 1]
        dve('scalar_tensor_tensor', ['xt%d' % s, 'stat%d' % t, 'bc'], ['xn%d' % s], out=xn[:, s], in0=xt[:, s],
            scalar=rsc, in1=Abc[l], op0=ALU.mult, op1=ALU.mult)
        dve('tensor_tensor', ['xn%d' % s, 'bc'], ['xn%d' % s], out=xn[:, s], in0=xn[:, s], in1=Bbf[l], op=ALU.add)

    def a_tr(t):
        s = t % 2
        b = bank()
        pbank[t] = b
        pb = ps[b][:].bitcast(BF16)
        mms = [('T', pb[:, kc * 128:(kc + 1) * 128], xn[:, s, kc * 128:(kc + 1) * 128], identB[:]) for kc in range(8)]
        pe_group(mms, ['xn%d' % s, 'identB'], ['ps%d' % b])

    def a_evac(t):
        b = pbank[t]
        pb = ps[b][:].bitcast(BF16)
        tsl = slice(t * 128, (t + 1) * 128) if t < NT else slice((t - NT) * 128, (t - NT + 1) * 128)
        dstT = hT if t < NT else hcT
        act(dstT[:, :, tsl], pb.rearrange("p (k c) -> p k c", c=128), AF.Copy, ['ps%d' % b], ['hT%d' % t])

    vbank = {}

    def b_mm(t):
        b = bank()
        vbank[t] = b
        hf = hT_tile(t)
        mms = []
        for wc in range(2):
            for kc in range(8):
                mms.append((ps[b][:, wc * 256:(wc + 1) * 256], hf(kc), wch[:, wc, kc, :], kc == 0, kc == 7))
        pe_group(mms, ['hT%d' % t, 'w0', 'w1'], ['ps%d' % b])

    def b_evac(t):
        b = vbank[t]
        s = t % 2
        vr = vraw[:, s]
        act(vr, ps[b][:], AF.Identity, ['ps%d' % b], ['vraw%d' % s, 'lnst%d' % t], accum_out=lnst[:, 0, t:t + 1])
        act(Ltmp[:], vr, AF.Square, ['vraw%d' % s], ['Ltmp', 'lnst%d' % t], accum_out=lnst[:, 1, t:t + 1])

    def b_small(t):
        m_ = lnst[:, 2, t:t + 1]
        msq = lnst[:, 3, t:t + 1]
        var = lnst[:, 4, t:t + 1]
        dve('tensor_scalar_mul', ['lnst%d' % t], ['lnst%d' % t], out=m_, in0=lnst[:, 0, t:t + 1], scalar1=1.0 / 512)
        dve('tensor_tensor', ['lnst%d' % t], ['lnst%d' % t], out=msq, in0=m_, in1=m_, op=ALU.mult)
        dve('scalar_tensor_tensor', ['lnst%d' % t], ['lnst%d' % t], out=var, in0=lnst[:, 1, t:t + 1], scalar=1.0 / 512,
            in1=msq, op0=ALU.mult, op1=ALU.subtract)

    def b_rstd(t):
        rstd_from(lnst[:, 5, t:t + 1], lnst[:, 4, t:t + 1], 1.0, EPS, 'lnst%d' % t, 'lnst%d' % t)
        dve('scalar_tensor_tensor', ['lnst%d' % t], ['lnst%d' % t], out=lnst[:, 6, t:t + 1], in0=lnst[:, 2, t:t + 1],
            scalar=-1.0, in1=lnst[:, 5, t:t + 1], op0=ALU.mult, op1=ALU.mult)

    def b_norm(t):
        s = t % 2
        vr = vraw[:, s]
        act(vr, vr, AF.Identity, ['vraw%d' % s, 'lnst%d' % t], ['vraw%d' % s], scale=lnst[:, 5, t:t + 1],
            bias=lnst[:, 6, t:t + 1])

    def b_fin(t):
        s = t % 2
        vr = vraw[:, s]
        dve('tensor_tensor', ['vraw%d' % s, 'alng'], ['vraw%d' % s], out=vr, in0=vr, in1=alng[:], op=ALU.mult)
        dve('tensor_tensor', ['vraw%d' % s, 'alnb'], ['vn%d' % t], out=vn[:, t, :], in0=vr, in1=alnb[:], op=ALU.add)

    dbank = {}

    def d_mm(t):
        b = bank()
        dbank[t] = b
        hf = hT_tile(t)
        mms = []
        for wc in range(2):
            for kc in range(8):
                mms.append((ps[b][:, wc * 256:(wc + 1) * 256], hf(kc), wch[:, 2 + wc, kc, :], kc == 0, kc == 7))
        pe_group(mms, ['hT%d' % t, 'w2', 'w3'], ['ps%d' % b])

    def d_evac(t):
        b = dbank[t]
        if t % 2 == 0:
            act(vtok[:, t, :], ps[b][:], AF.Copy, ['ps%d' % b], ['vtok%d' % t])
        else:
            dve('tensor_copy', ['ps%d' % b], ['vtok%d' % t], out=vtok[:, t, :], in_=ps[b][:])

    for t in range(NTC + 3):
        if 0 <= t - 2 < NTC:
            d_evac(t - 2)
        if 0 <= t - 2 < NT:
            b_norm(t - 2)
        if 0 <= t - 1 < NT:
            b_mm(t - 1)
        if 1 < t + 1 < NTC:
            a_load(t + 1)
        if t < NTC:
            a_mod(t)
            a_tr(t)
        if 0 <= t - 1 < NTC:
            d_mm(t - 1)
        if 0 <= t - 1 < NT:
            b_evac(t - 1)
            b_small(t - 1)
        if t < NTC:
            a_evac(t)
        if 0 <= t - 1 < NT:
            b_rstd(t - 1)
        if 0 <= t - 2 < NT:
            b_fin(t - 2)

    load_w(0, 0, 256)
    load_w(1, 256, 256)
    load_w(3, ZB0, 256)

    dma('pool', vn_in.ap().rearrange("(t p) c -> p t c", p=128), vn[:, :, 256:512],
        ['vn%d' % t for t in range(NT)], ['vn_in'], 'vnout')
    if not _NOCC:
        P.dma('pool', lambda e: e.collective_compute(
            "AllGather", ALU.bypass, replica_groups=[[0, 1, 2, 3], [4, 5, 6, 7]],
            ins=[vn_in.ap().opt()], outs=[vn_all.ap().opt()]), ['vn_in'], ['vn_all'], 'ccA', inc=1)


    def scan_step(d, t, n, kvb, ctx):
        if ctx:
            for pr in range(2):
                dve('scalar_tensor_tensor', ['Sctx%d' % d, 'dec%d' % t, 'ps%d' % kvb], ['Sctx%d' % d],
                    out=Sctx[:, d, pr, :], in0=Sctx[:, d, pr, :], scalar=dec[:, pr, d, t:t + 1],
                    in1=ps[kvb][:, pr * 128:(pr + 1) * 128], op0=ALU.mult, op1=ALU.add)
            return
        cur, nxt = n % 2, (n + 1) % 2
        act(Sloc[:, d, :, t, :], Sst[:, d, cur, :, :], AF.Copy, ['Sst%d%d' % (d, cur)], ['Sloc%d_%d' % (d, t)])
        for pr in range(2):
            dve('scalar_tensor_tensor', ['Sst%d%d' % (d, cur), 'dec%d' % t, 'ps%d' % kvb], ['Sst%d%d' % (d, nxt)],
                out=Sst[:, d, nxt, pr, :], in0=Sst[:, d, cur, pr, :], scalar=dec[:, pr, d, t:t + 1],
                in1=ps[kvb][:, pr * 128:(pr + 1) * 128], op0=ALU.mult, op1=ALU.add)

    def kv_mm(kdec_ap, kkey, t):
        b = bank()
        mms = []
        for h in range(4):
            pr, sub = h // 2, h % 2
            mms.append((ps[b][sub * 64:(sub + 1) * 64, pr * 128:(pr + 1) * 128], kdec_ap[:, h * 64:(h + 1) * 64],
                        vtok[:, t, h * 128:(h + 1) * 128], True, True))
        pe_group(mms, [kkey, 'vtok%d' % t], ['ps%d' % b])
        return b

    seq = list(range(NT, NTC)) + list(range(NT))
    pos_of = {t: i for i, t in enumerate(seq)}

    def blk_of(t):
        if t >= NT:
            return list(range(NT, NTC)), True
        return list(range(4 * (t // 4), 4 * (t // 4) + 4)), False

    ebank = {}

    def e_lr(t):
        tiles, is_ctx = blk_of(t)
        ntok = 128 * len(tiles)
        b = bank()
        if is_ctx:
            rhs = lambda kc: hcT[:, kc, :]
        else:
            t0 = tiles[0]
            rhs = lambda kc, t0=t0: hT[:, kc, t0 * 128:t0 * 128 + 512]
        mms = [(ps[b][0:32, 0:ntok], wlr[:, kc, :], rhs(kc), kc == 0, kc == 7) for kc in range(8)]
        pe_group(mms, hkeys(tiles) + ['wlr'], ['ps%d' % b])
        act(lrT1[0:32, 0:ntok], ps[b][0:32, 0:ntok], AF.Copy, ['ps%d' % b], ['lrT1'])

    def e1(t):
        tiles, is_ctx = blk_of(t)
        ti = tiles.index(t)
        if ti == 0:
            e_lr(t)
        ls = pos_of[t] % 2
        b = bank()
        pe_group([(ps[b][:], lrT1[0:33, ti * 128:(ti + 1) * 128], w2aug[0:33, :], True, True)],
                 ['lrT1', 'w2aug'], ['ps%d' % b])
        act(Ltmp[:], ps[b][:], AF.Exp, ['ps%d' % b], ['Ltmp'], scale=-1.0)
        act(Lf[:], Ltmp[:], AF.Ln, ['Ltmp'], ['Lf'], bias=1.0, scale=1.0)
        dve('tensor_copy', ['Lf'], ['Lb%d' % ls], out=Lhl[:, ls, 0, :], in_=Lf[:])
        dve('tensor_tensor', ['Lf', 'Lb%d' % ls], ['Lb%d' % ls], out=Lhl[:, ls, 1, :], in0=Lf[:], in1=Lhl[:, ls, 0, :],
            op=ALU.subtract)

    def e3_pe(t):
        hf = hT_tile(t)
        bk = bank()
        ebank[('k', t)] = bk
        mms = [(ps[bk][:, 0:256], hf(kc), wch[:, 1, kc, :], kc == 0, kc == 7) for kc in range(8)]
        pe_group(mms, ['hT%d' % t, 'w1'], ['ps%d' % bk])

    def e2(t):
        tiles, is_ctx = blk_of(t)
        ti = tiles.index(t)
        ls = pos_of[t] % 2
        bc_ = bank()
        mms = []
        for d in range(2):
            for hf_ in range(2):
                o_ap = ps[bc_][:, (d * 2 + hf_) * 128:(d * 2 + hf_ + 1) * 128]
                cs_ = slice(d * 256 + hf_ * 128, d * 256 + (hf_ + 1) * 128)
                mms.append((o_ap, Lhl[:, ls, 0, cs_], cst[:, d, :], True, False))
                mms.append((o_ap, Lhl[:, ls, 1, cs_], cst[:, d, :], False, True))
        pe_group(mms, ['Lb%d' % ls, 'cst'], ['ps%d' % bc_])
        br_ = bank()
        mms = []
        for d in range(2):
            o_ap = ps[br_][:, d * 256:(d + 1) * 256]
            mms.append((o_ap, cst[:, 2 + d, :], Lhl[:, ls, 0, d * 256:(d + 1) * 256], True, False))
            mms.append((o_ap, cst[:, 2 + d, :], Lhl[:, ls, 1, d * 256:(d + 1) * 256], False, True))
        pe_group(mms, ['Lb%d' % ls, 'cst'], ['ps%d' % br_])
        cv = ps[bc_][:].rearrange("p (a c) -> p a c", c=128)
        act(Xb[:, ls], ps[br_][:], AF.Exp, ['ps%d' % br_], ['Xb%d' % ls])
        act(dec[:, :, 0, t], cv[:, 0:2, 127], AF.Exp, ['ps%d' % bc_], ['dec%d' % t])
        act(dec[:, :, 1, t], cv[:, 2:4, 0], AF.Exp, ['ps%d' % bc_], ['dec%d' % t])
        if not is_ctx:
            for d in range(2):
                act(Ep[:, d, :, ti * 128:(ti + 1) * 128], cv[:, d * 2:d * 2 + 2, :], AF.Exp, ['ps%d' % bc_], ['Ep'])
                act(Em[:, d, :, ti * 128:(ti + 1) * 128], cv[:, d * 2:d * 2 + 2, :], AF.Exp, ['ps%d' % bc_], ['Em'],
                    scale=-1.0)

    def e3_dve(t):
        ls = pos_of[t] % 2
        bk = ebank[('k', t)]
        dve('tensor_tensor', ['ps%d' % bk, 'Xb%d' % ls], ['kdf%d' % ls], out=kdec_f[:, ls, :], in0=ps[bk][:, 0:256],
            in1=Xb[:, ls, 0:256], op=ALU.mult)
        dve('tensor_tensor', ['ps%d' % bk, 'Xb%d' % ls], ['kdb%d' % t], out=kdec_b[:, t, :], in0=ps[bk][:, 0:256],
            in1=Xb[:, ls, 256:512], op=ALU.mult)

    def e4(t):
        ls = pos_of[t] % 2
        kvb = kv_mm(kdec_f[:, ls, :], 'kdf%d' % ls, t)
        if t >= NT:
            scan_step(0, t, t - NT, kvb, True)
        else:
            scan_step(0, t, t, kvb, False)
            dve('tensor_tensor', ['Pn', 'dec%d' % t], ['Pn'], out=Pn[:, :, 0, t + 1], in0=Pn[:, :, 0, t],
                in1=dec[:, :, 0, t], op=ALU.mult)

    def e5(t):
        tiles, _ = blk_of(t)
        tb = tiles[0] * 128
        for mi in range(4):
            b = bank()
            slot, co = mi // 2, (mi % 2) * 128
            mms = [(ps[b][:], wch[:, slot, kc, co:co + 128], hT[:, kc, tb:tb + 512], kc == 0, kc == 7)
                   for kc in range(8)]
            pe_group(mms, hkeys(tiles) + ['w%d' % slot], ['ps%d' % b])
            hfi = mi % 2
            for d in range(2):
                if mi < 2:
                    dve('scalar_tensor_tensor', ['ps%d' % b, 'Ep'], ['qd%d' % tt for tt in tiles] + ['bc'],
                        out=qd[:, d, hfi, tb:tb + 512], in0=ps[b][:], scalar=0.125, in1=Ep[:, d, hfi, :],
                        op0=ALU.mult, op1=ALU.mult)
                else:
                    dve('tensor_tensor', ['ps%d' % b, 'Em'], ['kd%d' % tt for tt in tiles] + ['bc'],
                        out=kd[:, d, hfi, tb:tb + 512], in0=ps[b][:], in1=Em[:, d, hfi, :], op=ALU.mult)

    nseq = len(seq)
    for i in range(-2, nseq):
        if i + 2 < nseq:
            e1(seq[i + 2])
        if 0 <= i + 1 < nseq:
            t1_ = seq[i + 1]
            e3_pe(t1_)
            e2(t1_)
            e3_dve(t1_)
        if 0 <= i < nseq:
            e4(seq[i])
        if 0 <= i + 1 < nseq:
            t1_ = seq[i + 1]
            if t1_ < NT and t1_ % 4 == 3:
                e5(t1_)

    for n, t in enumerate([NT + 1, NT]):
        kvb = kv_mm(kdec_b[:, t, :], 'kdb%d' % t, t)
        scan_step(1, t, n, kvb, True)
    for n, t in enumerate(range(NT - 1, -1, -1)):
        kvb = kv_mm(kdec_b[:, t, :], 'kdb%d' % t, t)
        scan_step(1, t, n, kvb, False)
        dve('tensor_tensor', ['Pn', 'dec%d' % t], ['Pn'], out=Pn[:, :, 1, t], in0=Pn[:, :, 1, t + 1], in1=dec[:, :, 1, t],
            op=ALU.mult)

    for d in range(2):
        dve('tensor_copy', ['Sst%d0' % d], ['stg'], out=stg[:, d, :, 0:128], in_=Sst[:, d, 0, :, :])
    dve('tensor_copy', ['Pn'], ['stg'], out=stg[:, 0, :, 128], in_=Pn[:, :, 0, NT])
    dve('tensor_copy', ['Pn'], ['stg'], out=stg[:, 1, :, 128], in_=Pn[:, :, 1, 0])
    load_w(0, ZB0 + 256, 256)
    load_w(1, ZA0, 256)
    load_w(2, ZA0 + 256, 256)
    dma('pool', st_in.ap(), stg[:].rearrange("p a b c -> p (a b c)"), ['stg'], ['st_in'], 'stout')
    if not _NOCC:
        P.dma('pool', lambda e: e.collective_compute(
            "AllGather", ALU.bypass, replica_groups=[[0, 1, 2, 3], [4, 5, 6, 7]],
            ins=[st_in.ap().opt()], outs=[st_all.ap().opt()]), ['st_in'], ['st_all'], 'ccB', inc=1)

    if stop is not None:
        if end_phase(1, locals()):
            esC.close(); esX.close(); esB.close(); esA.close(); P.es.close()
            return nc
    esC.close()
    esX.close()

    esC = ExitStack()
    zb = alloc(esC, "zb", [128, 4, 2048], BF16, R)
    sv = alloc(esC, "sv", [128, 4, 2048], BF16, R)
    esC2 = ExitStack()
    stall = alloc(esC2, "stall", [128, 4, 2, 2, 132], F32, R)
    Sinit = alloc(esC2, "Sinit", [128, 2, 2, 128], F32, R)
    ftmp = alloc(esC2, "ftmp", [128, 2, 2, 128], F32, R)
    scm = alloc(esC2, "scm", [128, 2, 2, 4, 128], BF16, R)
    sqb = alloc(esC2, "sqb", [128, 2, 512], BF16, R)
    rsd = alloc(esC2, "rsd", [128, 2, 512], F32, R)
    t1 = alloc(esC2, "t1", [128, 2, 512], F32, R)


    def phase2_body():
        if _P2CUT < 1:
            return
        dma('sp', stall[:].rearrange("p r a b c -> p r (a b c)"), st_all.ap().rearrange("(r p) c -> p r c", p=128),
            ['st_all'], ['stall'], 'stin')
        for mi in range(4):
            slot, co = (3, 0)[mi // 2], (mi % 2) * 128
            for tbi in range(4):
                b = bank()
                tb = tbi * 512
                mms = [(ps[b][:], wch[:, slot, kc, co:co + 128], hT[:, kc, tb:tb + 512], kc == 0, kc == 7)
                       for kc in range(8)]
                pe_group(mms, hkeys(range(tbi * 4, tbi * 4 + 4)) + ['w%d' % slot], ['ps%d' % b])
                act(zb[:, mi, tb:tb + 512], ps[b][:], AF.Silu, ['ps%d' % b],
                    ['zb%d_%d' % (mi, tbi), 'xt0', 'xt1', 'xn0', 'xn1', 'hT16', 'hT17'])
        P.barrier(exclude=('ccB', 'stout', 'wl0', 'wl1', 'wl2', 'wl3'))

        def fproj(slot, mis, evac):
            for mi_l in range(2):
                mi = mis[mi_l]
                co = mi_l * 128
                for tbi in range(4):
                    b = bank()
                    tb = tbi * 512
                    mms = [(ps[b][:], wch[:, slot, kc, co:co + 128], hT[:, kc, tb:tb + 512], kc == 0, kc == 7)
                           for kc in range(8)]
                    pe_group(mms, hkeys(range(tbi * 4, tbi * 4 + 4)) + ['w%d' % slot], ['ps%d' % b])
                    evac(b, mi, tbi, tb)

        def ev_z(b, mi, tbi, tb):
            act(sv[:, mi, tb:tb + 512], ps[b][:], AF.Silu, ['ps%d' % b], ['sv%d_%d' % (mi, tbi)])

        def ev_u(b, mi, tbi, tb):
            key = 'sv%d_%d' % (mi, tbi)
            dve('tensor_tensor', ['ps%d' % b, key], [key], out=sv[:, mi, tb:tb + 512], in0=ps[b][:],
                in1=sv[:, mi, tb:tb + 512], op=ALU.mult)

        load_w(3, UA0, 256)
        load_w(0, UA0 + 256, 256)
        fproj(1, (0, 1), ev_z)
        fproj(2, (2, 3), ev_z)
        for d in range(2):
            for r in range(4):
                msk = fsel[:, d * 4 + r:d * 4 + r + 1]
                cmsk = fsel[:, 8 + d * 4 + r:8 + d * 4 + r + 1]
                dve('tensor_scalar_mul', ['stall', 'fsel'], ['stall'], out=stall[:, r, d, :, 0:128],
                    in0=stall[:, r, d, :, 0:128], scalar1=msk)
                dve('tensor_scalar', ['stall', 'fsel'], ['stall'], out=stall[:, r, d, :, 128:129],
                    in0=stall[:, r, d, :, 128:129], scalar1=msk, scalar2=cmsk, op0=ALU.mult, op1=ALU.add)
        fproj(3, (0, 1), ev_u)
        fproj(0, (2, 3), ev_u)

        browS = w2aug[0:1, 0:256].rearrange("p (g i) -> p g i", g=2)
        P.dma('pool', lambda e: e.dma_start(out=browS, in_=bsrows_d[0:1, 0:2, 0:128]), [], ['w2aug'], 'browl')
        for g in range(2):
            for q in range(4):
                b = bank()
                mms = []
                for i in range(4):
                    o_ap = ps[b][:, i * 128:(i + 1) * 128]
                    mms.append((o_ap, onesB[0:1, :], browS[0:1, g, :], True, False))
                    mms.append((o_ap, vn[:, q * 4 + i, g * 128:(g + 1) * 128], wsT[:, g, :], False, True))
                pe_group(mms, ['vn%d' % (q * 4 + i) for i in range(4)] + ['wsT', 'w2aug', 'onesB'], ['ps%d' % b])
                dve('tensor_tensor', ['ps%d' % b, 'sv%d_%d' % (g, q)], ['sv%d_%d' % (g, q)],
                    out=sv[:, g, q * 512:(q + 1) * 512], in0=ps[b][:], in1=sv[:, g, q * 512:(q + 1) * 512], op=ALU.mult)
        gv_ = vn_all.ap().rearrange("(r w) c -> r w c", w=64)
        allvn = ['vn%d' % tt for tt in range(NT)]
        dma('sp', vn[:, :, 256:512], gv_[:, 0:16, :], ['vn_all'], allvn, 'gathl0')
        dma('sp', vn[:, :, 0:256], gv_[:, 16:32, :], ['vn_all'], allvn, 'gathl1')

        for d in range(2):
            dve('tensor_copy', ['Sctx%d' % d], ['Sinit%d' % d], out=Sinit[:, d], in_=Sctx[:, d])
        for step in range(4):
            for d in range(2):
                r = step if d == 0 else 3 - step
                for pr in range(2):
                    dve('scalar_tensor_tensor', ['Sinit%d' % d, 'stall'], ['Sinit%d' % d], out=Sinit[:, d, pr],
                        in0=Sinit[:, d, pr], scalar=stall[:, r, d, pr, 128:129], in1=stall[:, r, d, pr, 0:128],
                        op0=ALU.mult, op1=ALU.add)
        if _P2CUT < 2:
            return
        bo_of = {}

        SPv = stall[:].rearrange("p r a b c -> p (r a b c)").bitcast(BF16)[:, 0:1024].rearrange(
            "p (s d r e) -> p s d r e", s=2, d=2, r=2)

        def fixup(t):
            sl = t % 2
            for d in range(2):
                pidx = t if d == 0 else t + 1
                for pr in range(2):
                    dve('tensor_scalar_mul', ['Sinit%d' % d, 'Pn'], ['SP%d' % sl, 'stall'], out=SPv[:, sl, d, pr, :],
                        in0=Sinit[:, d, pr], scalar1=Pn[:, pr, d, pidx:pidx + 1])

        def stA(t):
            s = t % 2
            tk = slice(t * 128, (t + 1) * 128)
            fixup(t)
            bx, by = bank(), bank()
            mms = []
            for d in range(2):
                for h in range(4):
                    pr, sub = h // 2, h % 2
                    bb_ = bx if sub == 0 else by
                    col = (d * 2 + pr) * 128
                    mms.append((ps[bb_][:, col:col + 128], kd[sub * 64:(sub + 1) * 64, d, pr, tk],
                                qd[sub * 64:(sub + 1) * 64, d, pr, tk], True, True))
            pe_group(mms, ['kd%d' % t, 'qd%d' % t], ['ps%d' % bx, 'ps%d' % by])
            for d in range(2):
                for sub, bb_ in ((0, bx), (1, by)):
                    dve('tensor_tensor', ['ps%d' % bb_, 'M4'], ['scm%d%d' % (s, d)], out=scm[:, s, d, sub::2, :],
                        in0=ps[bb_][:, d * 256:(d + 1) * 256].rearrange("p (h i) -> p h i", i=128),
                        in1=M4[:, d, 0:2, :], op=ALU.mult)

        def stB(t):
            s = t % 2
            tk = slice(t * 128, (t + 1) * 128)
            bo = bank()
            bo_of[t] = bo
            mms = []
            for h in range(4):
                pr, sub = h // 2, h % 2
                o_ap = ps[bo][:, h * 128:(h + 1) * 128]
                mms.append((o_ap, vtok[:, t, h * 128:(h + 1) * 128], scm[:, s, 0, h, :], True, False))
                mms.append((o_ap, vtok[:, t, h * 128:(h + 1) * 128], scm[:, s, 1, h, :], False, False))
                mms.append((o_ap, Sloc[sub * 64:(sub + 1) * 64, 0, pr, t, :], qd[sub * 64:(sub + 1) * 64, 0, pr, tk],
                            False, False))
                mms.append((o_ap, Sloc[sub * 64:(sub + 1) * 64, 1, pr, t, :], qd[sub * 64:(sub + 1) * 64, 1, pr, tk],
                            False, False))
                mms.append((o_ap, SPv[sub * 64:(sub + 1) * 64, s, 0, pr, :], qd[sub * 64:(sub + 1) * 64, 0, pr, tk],
                            False, False))
                mms.append((o_ap, SPv[sub * 64:(sub + 1) * 64, s, 1, pr, :], qd[sub * 64:(sub + 1) * 64, 1, pr, tk],
                            False, True))
            pe_group(mms, ['vtok%d' % t, 'scm%d0' % s, 'scm%d1' % s, 'Sloc0_%d' % t, 'Sloc1_%d' % t, 'qd%d' % t,
                           'SP%d' % s], ['ps%d' % bo])
            act(sqb[:, s], ps[bo][:], AF.Square, ['ps%d' % bo], ['sqb%d' % s])
            act(t1[:, s], ps[bo][:], AF.Copy, ['ps%d' % bo], ['t1%d' % s])

        def stC(t):
            s = t % 2
            tk = slice(t * 128, (t + 1) * 128)
            bo = bo_of[t]
            bs_ = bank()
            pe_group([(ps[bs_][:], onesB[:], sqb[:, s], True, True)], ['sqb%d' % s, 'onesB'], ['ps%d' % bs_])
            act(rsd[:, s], ps[bs_][:], AF.Ln, ['ps%d' % bs_], ['rsd%d' % s], bias=128.0 * EPS, scale=1.0)
            act(rsd[:, s], rsd[:, s], AF.Exp, ['rsd%d' % s], ['rsd%d' % s], scale=-0.5)
            dve('tensor_tensor', ['t1%d' % s, 'rsd%d' % s], ['t1%d' % s], out=t1[:, s], in0=t1[:, s], in1=rsd[:, s],
                op=ALU.mult)
            zkeys = ['zb%d_%d' % (h, t // 4) for h in range(4)]
            dve('tensor_tensor', ['t1%d' % s] + zkeys, zkeys, out=zb[:, :, tk],
                in0=t1[:, s].rearrange("p (h i) -> p h i", i=128), in1=zb[:, :, tk], op=ALU.mult)

        stA(0)
        for t in range(NT):
            if t + 1 < NT:
                stA(t + 1)
            stB(t)
            if t >= 1:
                stC(t - 1)
        stC(NT - 1)

    phase2_body()
    if stop is not None:
        if end_phase(2, locals()):
            esC2.close(); esC.close(); esB.close(); esA.close(); P.es.close()
            return nc
    esC2.close()
    esB.close()

    esB = ExitStack()
    gath = alloc(esB, "gath", [128, 64, 256], BF16)
    mT = alloc(esB, "mT", [128, 8, 2048], BF16)
    esC2 = ExitStack()
    wout = alloc(esC2, "wout", [128, 8, 1024], BF16, R)
    browc = wch[0:1, 3, 0:4, :].rearrange("p a b -> p (a b)").rearrange("p (g n) -> p g n", g=2)
    P.dma('pool', lambda e: e.dma_start(out=browc, in_=bsrows_d[0:1, 2:4, :]), [], ['w3'], 'browcl')
    fng = alloc(esC2, "fng", [128, 1024], F32, R)
    szb = alloc(esC2, "szb", [128, 2, 512], BF16, R)
    sg = alloc(esC2, "sg", [128, 2, 2, 512], F32, R)
    ost = alloc(esC2, "ost", [128, 16], F32, R)

    def colmix(wq):
        for g in range(2):
            svr = sv[:, 2 + g, :].rearrange("p (r w) -> p r w", w=64)
            b = bank()
            pcol = ps[b][:].rearrange("p (r w) -> p w r", w=16)
            mms = [(ps[b][:], onesB[0:1, :], browc[0:1, g, :], True, False)]
            if wq == 0:
                src = lambda wi: vn[:, wi, 256 + g * 128:256 + (g + 1) * 128]
            elif wq == 1:
                src = lambda wi: vn[:, wi, g * 128:(g + 1) * 128]
            else:
                src = lambda wi: gath[:, wq * 16 + wi, g * 128:(g + 1) * 128]
            mms += [(pcol[:, wi, :], src(wi), wsTcol[:, g, :], False, wi == 15) for wi in range(16)]
            gkeys = ['vn%d' % tt for tt in range(NT)] if wq < 2 else ['gath%d' % wq]
            pe_group(mms, gkeys + ['wsTcol', 'w3', 'onesB'], ['ps%d' % b])
            keys = ['sv%d_%d' % (2 + g, q) for q in range(4)]
            dve('tensor_tensor', ['ps%d' % b] + keys, keys, out=svr[:, :, wq * 16:(wq + 1) * 16],
                in0=ps[b][:].rearrange("p (r w) -> p r w", w=16), in1=svr[:, :, wq * 16:(wq + 1) * 16], op=ALU.mult)


    colmix(0)
    colmix(1)
    gv = vn_all.ap().rearrange("(r w) c -> r w c", w=64)
    for wq in range(2, 4):
        dma('sp', gath[:, wq * 16:(wq + 1) * 16, :], gv[:, wq * 16:(wq + 1) * 16, :], ['vn_all'], ['gath%d' % wq] + ['kd%d' % tt for tt in range(NT)] + ['qd%d' % tt for tt in range(NT)],
            'gathl%d' % wq)

    P.barrier()
    P.dma_group('sp', [(lambda e: e.dma_start(out=fng[:], in_=fng_d[:, :]), [], ['fng'])], 'setup')
    dve('memset', [], ['ost'], ap=ost[:], constant=0.0)
    colmix(2)
    colmix(3)

    wpab = gath[:].rearrange("p w c -> p (w c)")
    wpa_s = wpab[:, 0:4096].rearrange("p (f n) -> p f n", n=1024)
    wpb_s = wpab[:, 4096:8192].rearrange("p (f n) -> p f n", n=1024)
    P.dma('pool', lambda e: e.dma_start(out=wpa_s, in_=wpa_d.rearrange("(f p) n -> p f n", p=128)),
          [], ['gathA'], 'wpal')
    P.dma('pool', lambda e: e.dma_start(out=wpb_s, in_=wpb_d.rearrange("(f p) n -> p f n", p=128)),
          [], ['gathA'], 'wpbl')
    for h in range(4):
        act(wpb_s[:, h, :], wpb_s[:, h, :], AF.Copy, ['gathA', 'gs'], ['gathA'], scale=gs[:, h:h + 1])
    for i in range(8):
        slot = i % 2
        P.dma('pool', lambda e, i=i, slot=slot: e.dma_start(
            out=wch[:, slot, :, 0:128], in_=win_v[:, :, G0 + i * 128:G0 + (i + 1) * 128]), [], ['w%d' % slot],
            'wl%d' % slot)
        P.dma('pool', lambda e, i=i, slot=slot: e.dma_start(
            out=wch[:, slot, :, 128:256], in_=win_v[:, :, G0 + 1024 + i * 128:G0 + 1024 + (i + 1) * 128]), [],
            ['w%d' % slot], 'wl%d' % slot)
        for tbi in range(4):
            tb = tbi * 512
            hk = hkeys(range(tbi * 4, tbi * 4 + 4))
            s = tbi % 2
            bg = []
            for gi in range(2):
                b = bank()
                mms = [(ps[b][:], wch[:, slot, kc, gi * 128:(gi + 1) * 128], hT[:, kc, tb:tb + 512], kc == 0, kc == 7)
                       for kc in range(8)]
                pe_group(mms, hk + ['w%d' % slot], ['ps%d' % b])
                act(sg[:, s, gi], ps[b][:], AF.Sigmoid, ['ps%d' % b], ['sg%d%d' % (s, gi)])
                bg.append(b)
            ba = bank()
            mms = [(ps[ba][:], wpa_s[:, fc, i * 128:(i + 1) * 128], sv[:, fc, tb:tb + 512], fc == 0, fc == 3)
                   for fc in range(4)]
            pe_group(mms, ['gathA'] + ['sv%d_%d' % (fc, tbi) for fc in range(4)], ['ps%d' % ba])
            bb = bank()
            mms = [(ps[bb][:], wpb_s[:, fc, i * 128:(i + 1) * 128], zb[:, fc, tb:tb + 512], fc == 0, fc == 3)
                   for fc in range(4)]
            pe_group(mms, ['gathA'] + ['zb%d_%d' % (fc, tbi) for fc in range(4)], ['ps%d' % bb])
            dve('tensor_tensor', ['ps%d' % ba, 'sg%d0' % s], ['sg%d0' % s], out=sg[:, s, 0], in0=ps[ba][:],
                in1=sg[:, s, 0], op=ALU.mult)
            dve('tensor_tensor', ['ps%d' % bb, 'sg%d1' % s], ['sg%d1' % s], out=sg[:, s, 1], in0=ps[bb][:],
                in1=sg[:, s, 1], op=ALU.mult)
            dve('tensor_tensor', ['sg%d0' % s, 'sg%d1' % s], ['mT%d' % tbi], out=mT[:, i, tb:tb + 512],
                in0=sg[:, s, 0], in1=sg[:, s, 1], op=ALU.add)
        if i == 1:
            P.dma('pool', lambda e: e.dma_start(out=wout[:], in_=wout_d.rearrange("(kc p) n -> p kc n", p=128)),
                  [], ['wout'], 'woutl')
        if i == 4:
            for kc in range(8):
                dve('tensor_tensor', ['wout', 'gate_bc'], ['wout'], out=wout[:, kc, :], in0=wout[:, kc, :],
                    in1=gate_bc[:], op=ALU.mult)

    hTf = hT[:].rearrange("p k t -> p (k t)").bitcast(F32)
    for t in range(NT):
        s = t % 2
        xr = hTf[:, s * 1024:(s + 1) * 1024]
        xo = hTf[:, 2048 + s * 1024:2048 + (s + 1) * 1024]
        allh = hkeys(range(NT)) if t < 2 else []
        dma('sp', xr, x_d[t * 128:(t + 1) * 128, :], [], allh + ['xr%d' % s], 'xr%d' % s)
        for hf_ in range(2):
            b = bank()
            mms = [(ps[b][:], mT[:, kc, t * 128:(t + 1) * 128], wout[:, kc, hf_ * 512:(hf_ + 1) * 512], kc == 0, kc == 7)
                   for kc in range(8)]
            pe_group(mms, ['mT%d' % (t // 4), 'wout'], ['ps%d' % b])
            dve('tensor_tensor', ['ps%d' % b, 'xr%d' % s] + allh, ['xo%d' % s] + allh,
                out=xo[:, hf_ * 512:(hf_ + 1) * 512], in0=ps[b][:], in1=xr[:, hf_ * 512:(hf_ + 1) * 512],
                op=ALU.add)
        act(xr, xo, AF.Square, ['xo%d' % s], ['xr%d' % s, 'ost'], accum_out=ost[:, t:t + 1])
        rstd_from(ost[:, t:t + 1], ost[:, t:t + 1], 1.0 / 1024, EPS, 'ost', 'ost')
        dve('scalar_tensor_tensor', ['xo%d' % s, 'ost', 'fng'], ['xo%d' % s], out=xo, in0=xo, scalar=ost[:, t:t + 1],
            in1=fng[:], op0=ALU.mult, op1=ALU.mult)
        dma('pool', out_d[t * 128:(t + 1) * 128, :], xo, ['xo%d' % s], ['out'], 'ost%d' % s)

    end_phase(3, locals())
    esC2.close()
    esB.close()
    esC.close()
    esA.close()
    P.es.close()
    return nc


_NC_CACHE = {}


def _prep(inp):
    f = lambda a: np.ascontiguousarray(np.asarray(a, dtype=np.float32))
    x = f(inp['x']); c = f(inp['c']); ctx = f(inp['ctx']); c_ctx = f(inp['c_ctx'])
    w_mod = f(inp['w_mod'])[0]; b_mod = f(inp['b_mod'])[0]; norm_g = f(inp['norm_g'])[0]
    w_in = f(inp['w_in'])[0]; a_ln_g = f(inp['a_ln_g'])[0]; a_ln_b = f(inp['a_ln_b'])[0]
    a_ws = f(inp['a_ws'])[0]; a_bs = f(inp['a_bs'])[0]
    w2 = f(inp['b_gate_w2'])[0]; gb = f(inp['b_gate_b'])[0]; b_norm_g = f(inp['b_norm_g'])[0]
    wpa = f(inp['w_proj_a'])[0]; wpb = f(inp['w_proj_b'])[0]; wout = f(inp['w_out'])[0]
    fng = f(inp['final_norm_g'])

    def colT(v):
        return np.ascontiguousarray(v.reshape(-1, 128).T)

    j = np.arange(128)[:, None]; i = np.arange(128)[None, :]
    s = np.float32(-1.0 / 16.0)
    z = np.float32(0)
    consts = np.stack([
        np.eye(128, dtype=np.float32),
        np.where(j <= i, s, z), np.where(j >= i, s, z),
        np.where(j > i, s, z), np.where(j < i, s, z),
        (j <= i).astype(np.float32), (j >= i).astype(np.float32)]).astype(np.float32)
    w2aug = np.zeros((33, 512), np.float32)
    w2aug[0:16, 0:256] = w2[0]
    w2aug[16:32, 256:512] = w2[1]
    w2aug[32, 0:256] = gb[0]
    w2aug[32, 256:512] = gb[1]
    wsT = np.ascontiguousarray(np.transpose(a_ws, (0, 2, 1)))
    shared = {
        "cctxT": colT(c_ctx), "wmod": w_mod, "bmod": np.ascontiguousarray(b_mod[None, :]),
        "normgT": colT(norm_g), "win": w_in,
        "alng": np.ascontiguousarray(np.broadcast_to(a_ln_g[None, :], (128, 512))),
        "alnb": np.ascontiguousarray(np.broadcast_to(a_ln_b[None, :], (128, 512))),
        "wsT": np.ascontiguousarray(wsT[0:2]),
        "bsrow": np.ascontiguousarray(np.broadcast_to(a_bs[None, 0:2, :], (128, 2, 128))),
        "w2aug": w2aug, "bnormgT": np.ascontiguousarray(b_norm_g.reshape(4, 128).T),
        "wpa": wpa, "wpb": wpb, "wout": wout,
        "fng": np.ascontiguousarray(np.broadcast_to(fng[None, :], (128, 1024))),
        "consts": consts,
        "normgbc": np.ascontiguousarray(np.broadcast_to(norm_g[None, :], (128, 1024))),
    }
    in_maps = []
    for core in range(8):
        b, jj = core // 4, core % 4
        m = dict(shared)
        m["x"] = np.ascontiguousarray(x[b, 2048 * jj:2048 * (jj + 1), :])
        m["ctx"] = np.ascontiguousarray(ctx[b])
        m["cT"] = colT(c[b])
        m["wsTcol"] = np.ascontiguousarray(wsT[2:4, :, 32 * jj:32 * (jj + 1)])
        m["bscol"] = np.ascontiguousarray(np.broadcast_to(a_bs[None, 2:4, 32 * jj:32 * (jj + 1)], (128, 2, 32)))
        fs = np.zeros((128, 16), np.float32)
        for r in range(4):
            fs[:, r] = 1.0 if r < jj else 0.0
            fs[:, 4 + r] = 1.0 if r > jj else 0.0
            fs[:, 8 + r] = 0.0 if r < jj else 1.0
            fs[:, 12 + r] = 0.0 if r > jj else 1.0
        m["fsel"] = fs
        br = np.zeros((1, 4, 512), np.float32)
        for g in range(2):
            br[0, g] = np.tile(a_bs[g], 4)
            br[0, 2 + g] = np.repeat(a_bs[2 + g, 32 * jj:32 * (jj + 1)], 16)
        m["bsrows"] = br
        in_maps.append(m)
    return in_maps


def kernel(**inputs):
    in_maps = _prep(inputs)
    if 'nc' not in _NC_CACHE:
        _NC_CACHE['nc'] = build_nc()
    nc = _NC_CACHE['nc']
    res = run_bass_kernel_spmd(nc, in_maps, core_ids=list(range(8)))
    out = np.zeros((2, 8192, 1024), np.float32)
    for core in range(8):
        b, jj = core // 4, core % 4
        out[b, 2048 * jj:2048 * (jj + 1), :] = np.asarray(res.results[core]["out"], np.float32)
    return out
```

```python
import numpy as np
from contextlib import ExitStack
import concourse.bass as bass
import concourse.mybir as mybir
from concourse.bass_utils import run_bass_kernel_spmd

F32 = mybir.dt.float32
BF16 = mybir.dt.bfloat16
ALU = mybir.AluOpType
AF = mybir.ActivationFunctionType

EPS = 1e-6
NT = 16
NTC = 18
LR0, ZB0, UA0, VA0, ZA0, G0 = 1024, 1056, 1568, 2080, 2592, 3104
DEBUG = False
_P0CUT = 99
N_SILU_PASSES = 1
SAME_ENGINE_SYNC = True
_NOCC = False
_P2CUT = 99


class Prog:
    ENG = ('pe', 'act', 'dve', 'pool', 'sp')

    def __init__(self, nc):
        self.nc = nc
        self.ops = {e: [] for e in self.ENG}
        self.cnt = {e: 0 for e in self.ENG}
        self.dcnt = {}
        self.waited = {e: {} for e in self.ENG}
        self.lastw = {}
        self.readers = {}
        self.sem_h = {}
        self.es = ExitStack()

    def sem(self, k):
        if k not in self.sem_h:
            self.sem_h[k] = self.es.enter_context(self.nc.semaphore("s_" + k))
        return self.sem_h[k]

    def _deps(self, reads, writes):
        deps = {}

        def add(p):
            if p is None:
                return
            k, v = p
            if deps.get(k, 0) < v:
                deps[k] = v
        for k in reads:
            add(self.lastw.get(k))
        for k in writes:
            add(self.lastw.get(k))
            for r in self.readers.get(k, ()):
                add(r)
        return deps

    def _waits(self, eng, deps):
        out = []
        for k, v in deps.items():
            if self.waited[eng].get(k, 0) >= v:
                continue
            self.waited[eng][k] = v
            out.append((k, v))
        return out

    def _record(self, pos, reads, writes):
        for k in writes:
            self.lastw[k] = pos
            self.readers[k] = []
        for k in reads:
            if k not in writes:
                self.readers.setdefault(k, []).append(pos)

    def op(self, eng, fn, reads=(), writes=()):
        deps = self._deps(reads, writes)
        if eng == 'pe' or not SAME_ENGINE_SYNC:
            deps.pop(eng, None)
        waits = self._waits(eng, deps)
        self.cnt[eng] += 1
        self.sem(eng)
        self.ops[eng].append((waits, fn, (eng, 1)))
        self._record((eng, self.cnt[eng]), reads, writes)

    def dma(self, queue, fn, reads, writes, key, inc=16):
        deps = self._deps(reads, writes)
        waits = self._waits(queue, deps)
        self.dcnt[key] = self.dcnt.get(key, 0) + inc
        self.sem(key)
        self.ops[queue].append((waits, fn, (key, inc)))
        self._record((key, self.dcnt[key]), reads, writes)

    def dma_group(self, queue, items, key):
        allw = []
        for fn, reads, writes in items:
            deps = self._deps(reads, writes)
            waits = self._waits(queue, deps)
            self.dcnt[key] = self.dcnt.get(key, 0) + 16
            self.sem(key)
            self.ops[queue].append((waits, fn, (key, 16)))
            allw.append((reads, writes))
        pos = (key, self.dcnt[key])
        for reads, writes in allw:
            self._record(pos, reads, writes)

    def barrier(self, exclude=()):
        allsems = {e: self.cnt[e] for e in ('pe', 'act', 'dve', 'pool')}
        allsems.update(self.dcnt)
        for e in self.ENG:
            deps = {k: v for k, v in allsems.items() if v > 0 and k != e and k not in exclude}
            waits = self._waits(e, deps)
            if waits:
                self.ops[e].append((waits, None, None))

    def run(self):
        nc = self.nc
        for k in list(self.dcnt) + list(self.ENG):
            self.sem(k)
        with nc.Block() as block:
            def mk(e):
                def body(eng):
                    for waits, fn, sig in self.ops[e]:
                        for k, v in waits:
                            eng.wait_ge(self.sem(k), v)
                        if fn is None:
                            continue
                        ins = fn(eng)
                        ins.then_inc(self.sem(sig[0]), sig[1])
                return body
            block.tensor(mk('pe'))
            block.scalar(mk('act'))
            block.vector(mk('dve'))
            block.gpsimd(mk('pool'))
            block.sync(mk('sp'))
        for e in self.ENG:
            self.ops[e] = []


def build_nc(stop=None, dump_hook=None):
    nc = bass.Bass("TRN2", target_bir_lowering=False)
    P = Prog(nc)

    def end_phase(k, env, exclude=()):
        P.barrier(exclude)
        if dump_hook is not None:
            for name, ap in dump_hook(k, env):
                shp = list(ap.shape)
                dt_ = nc.dram_tensor("dbg_" + name, shp, ap.dtype, kind="ExternalOutput").ap()
                P.dma('sp', lambda e, dt_=dt_, ap=ap: e.dma_start(out=dt_, in_=ap), [], [], 'dbg')
            P.barrier()
        P.run()
        return stop == k

    def din(name, shape, dt=F32):
        return nc.dram_tensor(name, list(shape), dt, kind="ExternalInput").ap()

    x_d = din("x", [2048, 1024])
    ctx_d = din("ctx", [256, 1024])
    cT_d = din("cT", [128, 8])
    cctxT_d = din("cctxT", [128, 8])
    wmod_d = din("wmod", [1024, 3072])
    bmod_d = din("bmod", [1, 3072])
    normgT_d = din("normgT", [128, 8])
    win_d = din("win", [1024, 5152])
    alng_d = din("alng", [128, 512])
    alnb_d = din("alnb", [128, 512])
    wsT_d = din("wsT", [2, 128, 128])
    wsTcol_d = din("wsTcol", [2, 128, 32])
    bsrow_d = din("bsrow", [128, 2, 128])
    bscol_d = din("bscol", [128, 2, 32])
    w2aug_d = din("w2aug", [33, 512])
    bnormgT_d = din("bnormgT", [128, 4])
    wpa_d = din("wpa", [512, 1024])
    wpb_d = din("wpb", [512, 1024])
    wout_d = din("wout", [1024, 1024])
    fng_d = din("fng", [128, 1024])
    consts_d = din("consts", [7, 128, 128])
    fsel_d = din("fsel", [128, 16])
    normgbc_d = din("normgbc", [128, 1024])
    bsrows_d = din("bsrows", [1, 4, 512])
    out_d = nc.dram_tensor("out", [2048, 1024], F32, kind="ExternalOutput").ap()

    vn_in = nc.dram_tensor("vn_in", [2048, 256], BF16)
    vn_all = nc.dram_tensor("vn_all", [8192, 256], BF16)
    st_in = nc.dram_tensor("st_in", [128, 528], F32)
    st_all = nc.dram_tensor("st_all", [512, 528], F32)

    win_v = win_d.rearrange("(kc p) n -> p kc n", p=128)
    wmod_v = wmod_d.rearrange("(kc p) n -> p kc n", p=128)

    esA = ExitStack()
    _uid = []

    def alloc(es, name, shape, dt, side="left"):
        return es.enter_context(nc.sbuf_tensor("sb_" + name + "_%d" % len(_uid), list(shape), dt, side=side)) if not _uid.append(0) else None

    hT = alloc(esA, "hT", [128, 8, 2048], BF16)
    vn = alloc(esA, "vn", [128, NT, 512], BF16)
    wch = alloc(esA, "wch", [128, 4, 8, 256], BF16)
    identB = alloc(esA, "identB", [128, 128], BF16)
    onesB = alloc(esA, "onesB", [128, 128], BF16)
    M4 = alloc(esA, "M4", [128, 2, 4, 128], BF16)
    gate_bc = alloc(esA, "gate_bc", [128, 1024], F32)
    dec = alloc(esA, "dec", [128, 2, 2, NTC], F32)
    Pn = alloc(esA, "Pn", [128, 2, 2, NT + 1], F32)
    stat = alloc(esA, "stat", [128, 2, NTC], F32)
    w2aug = alloc(esA, "w2aug", [33, 512], BF16)
    wsT = alloc(esA, "wsT", [128, 2, 128], BF16)
    wsTcol = alloc(esA, "wsTcol", [128, 2, 32], BF16)
    bnormgT = alloc(esA, "bnormgT", [128, 4], F32)
    gs = alloc(esA, "gs", [128, 4], F32)
    fsel = alloc(esA, "fsel", [128, 16], F32)

    ps = [esA.enter_context(nc.psum_tensor("ps%d" % i, [128, 512], F32)) for i in range(8)]
    bank_ctr = [0]

    def bank():
        b = bank_ctr[0] % 8
        bank_ctr[0] += 1
        return b

    def st_col(kind, t):
        return kind * 18 + t

    def pe_group(mms, reads, writes):
        def fn(pe):
            ins = None
            for m in mms:
                if m[0] == 'T':
                    ins = pe.transpose(out=m[1], in_=m[2], identity=m[3])
                else:
                    ins = pe.matmul(m[0], lhsT=m[1], rhs=m[2], start=m[3], stop=m[4])
            return ins
        P.op('pe', fn, reads, writes)

    def act(out, in_, func, reads, writes, **kw):
        P.op('act', lambda e: e.activation(out=out, in_=in_, func=func, **kw), reads, writes)

    def dve(fname, reads, writes, **kw):
        P.op('dve', lambda e: getattr(e, fname)(**kw), reads, writes)

    def pool(fname, reads, writes, **kw):
        P.op('pool', lambda e: getattr(e, fname)(**kw), reads, writes)

    def dma(queue, out, in_, reads, writes, key):
        P.dma(queue, lambda e: e.dma_start(out=out, in_=in_), reads, writes, key)

    def load_w(slot, col0, ncols, dstcol=0):
        dma('pool', wch[:, slot, :, dstcol:dstcol + ncols], win_v[:, :, col0:col0 + ncols],
            [], ['w%d' % slot], 'wl%d' % slot)

    def rstd_from(out, in_, scale, bias, key_in, key_out):
        act(out, in_, AF.Ln, [key_in], [key_out], bias=bias, scale=scale)
        act(out, out, AF.Exp, [key_out], [key_out], scale=-0.5)

    dbg = []

    esB = ExitStack()
    qd = alloc(esB, "qd", [128, 2, 2, 2048], BF16)
    kd = alloc(esB, "kd", [128, 2, 2, 2048], BF16)
    vtok = alloc(esB, "vtok", [128, NTC, 512], BF16)
    Sloc = alloc(esB, "Sloc", [128, 2, 2, NT, 128], BF16)
    Sst = alloc(esB, "Sst", [128, 2, 2, 2, 128], F32)
    Sctx = alloc(esB, "Sctx", [128, 2, 2, 128], F32)
    stg = alloc(esB, "stg", [128, 2, 2, 132], F32)

    bcf = qd[:].rearrange("p a b t -> p (a b t)").bitcast(F32)
    Abc = [bcf[:, 0:1024], bcf[:, 2048:3072]]
    Bbc = [bcf[:, 1024:2048], bcf[:, 3072:4096]]
    kdf = kd[:].rearrange("p a b t -> p (a b t)")
    Bbf = [kdf[:, 0:1024], kdf[:, 1024:2048]]

    esX = ExitStack()
    xt = alloc(esX, "xt", [128, 2, 1024], F32, side="right")
    xn = alloc(esX, "xn", [128, 2, 1024], BF16, side="right")

    def a_load(t):
        s = t % 2
        src = x_d[t * 128:(t + 1) * 128, :] if t < NT else ctx_d[(t - NT) * 128:(t - NT + 1) * 128, :]
        dma('sp', xt[:, s], src, [], ['xt%d' % s], 'x%d' % s)
        ssc = stat[:, 0, t:t + 1]
        rsc = stat[:, 1, t:t + 1]
        act(xn[:, s], xt[:, s], AF.Square, ['xt%d' % s], ['xn%d' % s, 'stat%d' % t], accum_out=ssc)
        rstd_from(rsc, ssc, 1.0 / 1024, EPS, 'stat%d' % t, 'stat%d' % t)

    esR = ExitStack()
    wmc = alloc(esR, "wmc", [128, 2, 8, 512], BF16, side="right")
    bmr = alloc(esR, "bmr", [1, 2, 512], F32, side="right")
    modrep = alloc(esR, "modrep", [128, 2048], F32, side="right")
    stat_mix = alloc(esR, "stat_mix", [128, 2, 8, 128], BF16, side="right")
    stat_c = alloc(esR, "stat_c", [128, 2, 8, 128], BF16, side="right")
    schl = alloc(esR, "schl", [128, 2, 2, 8], F32, side="right")
    sc16 = alloc(esR, "sc16", [128, 2, 8], BF16, side="right")
    cTs = alloc(esR, "cTs", [128, 8], F32, side="right")
    cctxTs = alloc(esR, "cctxTs", [128, 8], F32, side="right")
    sc = alloc(esR, "sc", [128, 8], F32, side="right")
    identF = alloc(esR, "identF", [128, 128], F32, side="right")
    onesF = alloc(esR, "onesF", [128, 128], F32, side="right")
    normgbc = alloc(esR, "normgbc", [128, 1024], F32, side="right")
    scc = alloc(esR, "scc", [128, 8], F32, side="right")

    def phase0_body():
        items = [
            (lambda e: e.dma_start(out=identF[:], in_=consts_d[0, :, :]), [], ['identF']),
            (lambda e: e.dma_start(out=cTs[:], in_=cT_d[:, :]), [], ['cTs']),
            (lambda e: e.dma_start(out=cctxTs[:], in_=cctxT_d[:, :]), [], ['cctxTs']),
            (lambda e: e.dma_start(out=bnormgT[:], in_=bnormgT_d[:, :]), [], ['bnormgT']),
            (lambda e: e.dma_start(out=fsel[:], in_=fsel_d[:, :]), [], ['fsel']),
            (lambda e: e.dma_start(out=normgbc[:], in_=normgbc_d[:, :]), [], ['normgbc']),
        ]
        P.dma_group('sp', items[1:3], 'setup0')
        for ch in range(2):
            dma('pool', wmc[:, ch], wmod_v[:, :, ch * 512:(ch + 1) * 512], [], ['wmc%d' % ch], 'wm%d' % ch)
        P.dma_group('sp', items[:1] + items[3:], 'setup')
        citems = [
            (lambda e: e.dma_start(out=identB[:], in_=consts_d[0, :, :]), [], ['identB']),
            (lambda e: e.dma_start(out=w2aug[:], in_=w2aug_d[:, :]), [], ['w2aug']),
            (lambda e: e.dma_start(out=wsT[:], in_=wsT_d.rearrange("g p c -> p g c")), [], ['wsT']),
            (lambda e: e.dma_start(out=wsTcol[:], in_=wsTcol_d.rearrange("g p c -> p g c")), [], ['wsTcol']),
        ]
        for d in range(2):
            for h in range(4):
                citems.append((lambda e, d=d, h=h: e.dma_start(out=M4[:, d, h, :], in_=consts_d[5 + d, :, :]),
                               [], ['M4']))
        P.dma_group('pool', citems, 'setupc')

        if _P0CUT < 2:
            return
        dve('memset', [], ['onesB'], ap=onesB[:], constant=1.0)
        dve('memset', [], ['onesF'], ap=onesF[:], constant=1.0)
        dve('memset', [], ['Pn'], ap=Pn[:], constant=1.0)
        dve('memset', [], ['stat%d' % t for t in range(NTC)], ap=stat[:], constant=0.0)

        act(sc[:], cTs[:], AF.Silu, ['cTs'], ['sc'])
        act(scc[:], cctxTs[:], AF.Silu, ['cctxTs'], ['scc'])
        for j, src in enumerate((sc, scc)):
            dve('tensor_copy', ['sc', 'scc'], ['sc16'], out=sc16[:, j, :], in_=src[:])
            dve('tensor_copy', ['sc16'], ['schl'], out=schl[:, j, 0, :], in_=sc16[:, j, :])
            if N_SILU_PASSES > 1:
                dve('tensor_tensor', ['sc', 'scc', 'schl'], ['schl'], out=schl[:, j, 1, :], in0=src[:], in1=schl[:, j, 0, :],
                    op=ALU.subtract)
        for hl in range(N_SILU_PASSES):
            dve('tensor_copy', ['schl'], ['stat_mix'], out=stat_mix[:, hl, :, 0:64],
                in_=schl[:, 0, hl, :].unsqueeze(2).to_broadcast([128, 8, 64]))
            dve('tensor_copy', ['schl'], ['stat_mix'], out=stat_mix[:, hl, :, 64:128],
                in_=schl[:, 1, hl, :].unsqueeze(2).to_broadcast([128, 8, 64]))
            dve('tensor_copy', ['schl'], ['stat_c'], out=stat_c[:, hl, :, :],
                in_=schl[:, 0, hl, :].unsqueeze(2).to_broadcast([128, 8, 128]))

        if _P0CUT < 3:
            return
        for ch in range(6):
            slot = ch % 2
            c0 = ch * 512
            if ch >= 2:
                dma('pool', wmc[:, slot], wmod_v[:, :, c0:c0 + 512], [], ['wmc%d' % slot], 'wm%d' % slot)
            b = bank()
            st_t = stat_mix if ch < 4 else stat_c
            mms = []
            for hl in range(N_SILU_PASSES):
                for kc in range(8):
                    mms.append((ps[b][:], st_t[:, hl, kc, :], wmc[:, slot, kc, :], hl == 0 and kc == 0, False))
            dma('sp', bmr[0:1, slot, :], bmod_d[0:1, c0:c0 + 512], [], ['bmr%d' % slot], 'bm%d' % slot)
            mms.append((ps[b][:], onesF[0:1, :], bmr[0:1, slot, :], False, True))
            pe_group(mms, ['stat_mix', 'stat_c', 'wmc%d' % slot, 'bmr%d' % slot, 'onesF'], ['ps%d' % b])
            if ch < 4:
                act(modrep[:, c0:c0 + 512], ps[b][:], AF.Copy, ['ps%d' % b], ['modrep'])
            else:
                act(gate_bc[:, c0 - 2048:c0 - 2048 + 512], ps[b][:], AF.Copy, ['ps%d' % b], ['gate_bc'])
        load_w(0, VA0, 256)
        load_w(1, VA0 + 256, 256)
        load_w(2, 512, 256)
        load_w(3, 768, 256)
        if _P0CUT < 4:
            return
        for l, row in ((0, 0), (1, 64)):
            for n in range(4):
                b = bank()
                pe_group([(ps[b][:], onesF[row:row + 1, :], modrep[row:row + 1, n * 512:(n + 1) * 512], True, True)],
                         ['modrep', 'onesF'], ['ps%d' % b])
                if n < 2:
                    cs = slice(n * 512, (n + 1) * 512)
                    act(Bbf[l][:, cs], ps[b][:], AF.Copy, ['ps%d' % b], ['bc'])
                else:
                    dve('scalar_tensor_tensor', ['ps%d' % b, 'normgbc'], ['bc'],
                        out=Abc[l][:, (n - 2) * 512:(n - 1) * 512], in0=ps[b][:], scalar=1.0,
                        in1=normgbc[:, (n - 2) * 512:(n - 1) * 512], op0=ALU.add, op1=ALU.mult)
        dve('tensor_scalar_mul', ['bnormgT'], ['gs'], out=gs[:], in0=bnormgT[:], scalar1=float(np.sqrt(128.0)))
        a_load(0)
        a_load(1)

    phase0_body()
    if end_phase(0, locals(), exclude=('wl0', 'wl1', 'wl2', 'wl3') if stop is None else ()):
        esR.close(); esX.close(); esB.close(); esA.close(); P.es.close()
        return nc
    esR.close()

    esC = ExitStack()
    R = "right"
    hcT = alloc(esC, "hcT", [128, 8, 256], BF16, R)
    kdec_b = alloc(esC, "kdec_b", [128, NTC, 256], BF16, R)
    kdec_f = alloc(esC, "kdec_f", [128, 2, 256], BF16, R)
    alng = alloc(esC, "alng", [128, 512], F32, R)
    alnb = alloc(esC, "alnb", [128, 512], F32, R)
    vraw = alloc(esC, "vraw", [128, 2, 512], F32, R)
    Lf = alloc(esC, "Lf", [128, 512], F32, R)
    Lhl = alloc(esC, "Lhl", [128, 2, 2, 512], BF16, R)
    Ltmp = alloc(esC, "Ltmp", [128, 512], F32, R)
    Ep = alloc(esC, "Ep", [128, 2, 2, 512], BF16, R)
    Em = alloc(esC, "Em", [128, 2, 2, 512], BF16, R)
    Xb = alloc(esC, "Xb", [128, 2, 512], BF16, R)
    lrT1 = alloc(esC, "lrT1", [33, 512], BF16, R)
    lnst = alloc(esC, "lnst", [128, 8, NT], F32, R)
    wlr = alloc(esC, "wlr", [128, 8, 32], BF16, R)
    cst = alloc(esC, "cst", [128, 4, 128], BF16, R)

    P.dma_group('sp', [
        (lambda e: e.dma_start(out=alng[:], in_=alng_d[:, :]), [], ['alng']),
        (lambda e: e.dma_start(out=alnb[:], in_=alnb_d[:, :]), [], ['alnb']),
    ], 'setup')
    P.dma('pool', lambda e: e.dma_start(out=cst[:], in_=consts_d[1:5, :, :].rearrange("k p c -> p k c")), [], ['cst'],
          'cstl')
    dve('memset', [], ['Sst00', 'Sst01', 'Sst10', 'Sst11'], ap=Sst[:], constant=0.0)
    dve('memset', [], ['Sctx0', 'Sctx1'], ap=Sctx[:], constant=0.0)
    dve('memset', [], ['stg'], ap=stg[:], constant=0.0)
    dve('memset', [], ['lnst%d' % t for t in range(NT)], ap=lnst[:], constant=0.0)
    dve('memset', [], ['lrT1'], ap=lrT1[32:33, :], constant=1.0)

    dma('pool', wlr[:], win_v[:, :, LR0:LR0 + 32], [], ['wlr'], 'wlrl')

    def hkeys(tiles):
        return ['hT%d' % t for t in tiles]

    def hT_tile(t):
        if t < NT:
            return lambda kc: hT[:, kc, t * 128:(t + 1) * 128]
        return lambda kc: hcT[:, kc, (t - NT) * 128:(t - NT + 1) * 128]

    pbank = {}

    def a_mod(t):
        s = t % 2
        l = 0 if t < NT else 1
        rsc = stat[:, 1, t:t + 1]
        dve('scalar_tensor_tensor', ['xt%d' % s, 'stat%d' % t, 'bc'], ['xn%d' % s], out=xn[:, s], in0=xt[:, s],
            scalar=rsc, in1=Abc[l], op0=ALU.mult, op1=ALU.mult)
        dve('tensor_tensor', ['xn%d' % s, 'bc'], ['xn%d' % s], out=xn[:, s], in0=xn[:, s], in1=Bbf[l], op=ALU.add)

    def a_tr(t):
        s = t % 2
        b = bank()
        pbank[t] = b
        pb = ps[b][:].bitcast(BF16)
        mms = [('T', pb[:, kc * 128:(kc + 1) * 128], xn[:, s, kc * 128:(kc + 1) * 128], identB[:]) for kc in range(8)]
        pe_group(mms, ['xn%d' % s, 'identB'], ['ps%d' % b])

    def a_evac(t):
        b = pbank[t]
        pb = ps[b][:].bitcast(BF16)
        tsl = slice(t * 128, (t + 1) * 128) if t < NT else slice((t - NT) * 128, (t - NT + 1) * 128)
        dstT = hT if t < NT else hcT
        act(dstT[:, :, tsl], pb.rearrange("p (k c) -> p k c", c=128), AF.Copy, ['ps%d' % b], ['hT%d' % t])

    vbank = {}

    def b_mm(t):
        b = bank()
        vbank[t] = b
        hf = hT_tile(t)
        mms = []
        for wc in range(2):
            for kc in range(8):
                mms.append((ps[b][:, wc * 256:(wc + 1) * 256], hf(kc), wch[:, wc, kc, :], kc == 0, kc == 7))
        pe_group(mms, ['hT%d' % t, 'w0', 'w1'], ['ps%d' % b])

    def b_evac(t):
        b = vbank[t]
        s = t % 2
        vr = vraw[:, s]
        act(vr, ps[b][:], AF.Identity, ['ps%d' % b], ['vraw%d' % s, 'lnst%d' % t], accum_out=lnst[:, 0, t:t + 1])
        act(Ltmp[:], vr, AF.Square, ['vraw%d' % s], ['Ltmp', 'lnst%d' % t], accum_out=lnst[:, 1, t:t + 1])

    def b_small(t):
        m_ = lnst[:, 2, t:t + 1]
        msq = lnst[:, 3, t:t + 1]
        var = lnst[:, 4, t:t + 1]
        dve('tensor_scalar_mul', ['lnst%d' % t], ['lnst%d' % t], out=m_, in0=lnst[:, 0, t:t + 1], scalar1=1.0 / 512)
        dve('tensor_tensor', ['lnst%d' % t], ['lnst%d' % t], out=msq, in0=m_, in1=m_, op=ALU.mult)
        dve('scalar_tensor_tensor', ['lnst%d' % t], ['lnst%d' % t], out=var, in0=lnst[:, 1, t:t + 1], scalar=1.0 / 512,
            in1=msq, op0=ALU.mult, op1=ALU.subtract)

    def b_rstd(t):
        rstd_from(lnst[:, 5, t:t + 1], lnst[:, 4, t:t + 1], 1.0, EPS, 'lnst%d' % t, 'lnst%d' % t)
        dve('scalar_tensor_tensor', ['lnst%d' % t], ['lnst%d' % t], out=lnst[:, 6, t:t + 1], in0=lnst[:, 2, t:t + 1],
            scalar=-1.0, in1=lnst[:, 5, t:t + 1], op0=ALU.mult, op1=ALU.mult)

    def b_norm(t):
        s = t % 2
        vr = vraw[:, s]
        act(vr, vr, AF.Identity, ['vraw%d' % s, 'lnst%d' % t], ['vraw%d' % s], scale=lnst[:, 5, t:t + 1],
            bias=lnst[:, 6, t:t + 1])

    def b_fin(t):
        s = t % 2
        vr = vraw[:, s]
        dve('tensor_tensor', ['vraw%d' % s, 'alng'], ['vraw%d' % s], out=vr, in0=vr, in1=alng[:], op=ALU.mult)
        dve('tensor_tensor', ['vraw%d' % s, 'alnb'], ['vn%d' % t], out=vn[:, t, :], in0=vr, in1=alnb[:], op=ALU.add)

    dbank = {}

    def d_mm(t):
        b = bank()
        dbank[t] = b
        hf = hT_tile(t)
        mms = []
        for wc in range(2):
            for kc in range(8):
                mms.append((ps[b][:, wc * 256:(wc + 1) * 256], hf(kc), wch[:, 2 + wc, kc, :], kc == 0, kc == 7))
        pe_group(mms, ['hT%d' % t, 'w2', 'w3'], ['ps%d' % b])

    def d_evac(t):
        b = dbank[t]
        if t % 2 == 0:
            act(vtok[:, t, :], ps[b][:], AF.Copy, ['ps%d' % b], ['vtok%d' % t])
        else:
            dve('tensor_copy', ['ps%d' % b], ['vtok%d' % t], out=vtok[:, t, :], in_=ps[b][:])

    for t in range(NTC + 3):
        if 0 <= t - 2 < NTC:
            d_evac(t - 2)
        if 0 <= t - 2 < NT:
            b_norm(t - 2)
        if 0 <= t - 1 < NT:
            b_mm(t - 1)
        if 1 < t + 1 < NTC:
            a_load(t + 1)
        if t < NTC:
            a_mod(t)
            a_tr(t)
        if 0 <= t - 1 < NTC:
            d_mm(t - 1)
        if 0 <= t - 1 < NT:
            b_evac(t - 1)
            b_small(t - 1)
        if t < NTC:
            a_evac(t)
        if 0 <= t - 1 < NT:
            b_rstd(t - 1)
        if 0 <= t - 2 < NT:
            b_fin(t - 2)

    load_w(0, 0, 256)
    load_w(1, 256, 256)
    load_w(3, ZB0, 256)

    dma('pool', vn_in.ap().rearrange("(t p) c -> p t c", p=128), vn[:, :, 256:512],
        ['vn%d' % t for t in range(NT)], ['vn_in'], 'vnout')
    if not _NOCC:
        P.dma('pool', lambda e: e.collective_compute(
            "AllGather", ALU.bypass, replica_groups=[[0, 1, 2, 3], [4, 5, 6, 7]],
            ins=[vn_in.ap().opt()], outs=[vn_all.ap().opt()]), ['vn_in'], ['vn_all'], 'ccA', inc=1)


    def scan_step(d, t, n, kvb, ctx):
        if ctx:
            for pr in range(2):
                dve('scalar_tensor_tensor', ['Sctx%d' % d, 'dec%d' % t, 'ps%d' % kvb], ['Sctx%d' % d],
                    out=Sctx[:, d, pr, :], in0=Sctx[:, d, pr, :], scalar=dec[:, pr, d, t:t + 1],
                    in1=ps[kvb][:, pr * 128:(pr + 1) * 128], op0=ALU.mult, op1=ALU.add)
            return
        cur, nxt = n % 2, (n + 1) % 2
        act(Sloc[:, d, :, t, :], Sst[:, d, cur, :, :], AF.Copy, ['Sst%d%d' % (d, cur)], ['Sloc%d_%d' % (d, t)])
        for pr in range(2):
            dve('scalar_tensor_tensor', ['Sst%d%d' % (d, cur), 'dec%d' % t, 'ps%d' % kvb], ['Sst%d%d' % (d, nxt)],
                out=Sst[:, d, nxt, pr, :], in0=Sst[:, d, cur, pr, :], scalar=dec[:, pr, d, t:t + 1],
                in1=ps[kvb][:, pr * 128:(pr + 1) * 128], op0=ALU.mult, op1=ALU.add)

    def kv_mm(kdec_ap, kkey, t):
        b = bank()
        mms = []
        for h in range(4):
            pr, sub = h // 2, h % 2
            mms.append((ps[b][sub * 64:(sub + 1) * 64, pr * 128:(pr + 1) * 128], kdec_ap[:, h * 64:(h + 1) * 64],
                        vtok[:, t, h * 128:(h + 1) * 128], True, True))
        pe_group(mms, [kkey, 'vtok%d' % t], ['ps%d' % b])
        return b

    seq = list(range(NT, NTC)) + list(range(NT))
    pos_of = {t: i for i, t in enumerate(seq)}

    def blk_of(t):
        if t >= NT:
            return list(range(NT, NTC)), True
        return list(range(4 * (t // 4), 4 * (t // 4) + 4)), False

    ebank = {}

    def e_lr(t):
        tiles, is_ctx = blk_of(t)
        ntok = 128 * len(tiles)
        b = bank()
        if is_ctx:
            rhs = lambda kc: hcT[:, kc, :]
        else:
            t0 = tiles[0]
            rhs = lambda kc, t0=t0: hT[:, kc, t0 * 128:t0 * 128 + 512]
        mms = [(ps[b][0:32, 0:ntok], wlr[:, kc, :], rhs(kc), kc == 0, kc == 7) for kc in range(8)]
        pe_group(mms, hkeys(tiles) + ['wlr'], ['ps%d' % b])
        act(lrT1[0:32, 0:ntok], ps[b][0:32, 0:ntok], AF.Copy, ['ps%d' % b], ['lrT1'])

    def e1(t):
        tiles, is_ctx = blk_of(t)
        ti = tiles.index(t)
        if ti == 0:
            e_lr(t)
        ls = pos_of[t] % 2
        b = bank()
        pe_group([(ps[b][:], lrT1[0:33, ti * 128:(ti + 1) * 128], w2aug[0:33, :], True, True)],
                 ['lrT1', 'w2aug'], ['ps%d' % b])
        act(Ltmp[:], ps[b][:], AF.Exp, ['ps%d' % b], ['Ltmp'], scale=-1.0)
        act(Lf[:], Ltmp[:], AF.Ln, ['Ltmp'], ['Lf'], bias=1.0, scale=1.0)
        dve('tensor_copy', ['Lf'], ['Lb%d' % ls], out=Lhl[:, ls, 0, :], in_=Lf[:])
        dve('tensor_tensor', ['Lf', 'Lb%d' % ls], ['Lb%d' % ls], out=Lhl[:, ls, 1, :], in0=Lf[:], in1=Lhl[:, ls, 0, :],
            op=ALU.subtract)

    def e3_pe(t):
        hf = hT_tile(t)
        bk = bank()
        ebank[('k', t)] = bk
        mms = [(ps[bk][:, 0:256], hf(kc), wch[:, 1, kc, :], kc == 0, kc == 7) for kc in range(8)]
        pe_group(mms, ['hT%d' % t, 'w1'], ['ps%d' % bk])

    def e2(t):
        tiles, is_ctx = blk_of(t)
        ti = tiles.index(t)
        ls = pos_of[t] % 2
        bc_ = bank()
        mms = []
        for d in range(2):
            for hf_ in range(2):
                o_ap = ps[bc_][:, (d * 2 + hf_) * 128:(d * 2 + hf_ + 1) * 128]
                cs_ = slice(d * 256 + hf_ * 128, d * 256 + (hf_ + 1) * 128)
                mms.append((o_ap, Lhl[:, ls, 0, cs_], cst[:, d, :], True, False))
                mms.append((o_ap, Lhl[:, ls, 1, cs_], cst[:, d, :], False, True))
        pe_group(mms, ['Lb%d' % ls, 'cst'], ['ps%d' % bc_])
        br_ = bank()
        mms = []
        for d in range(2):
            o_ap = ps[br_][:, d * 256:(d + 1) * 256]
            mms.append((o_ap, cst[:, 2 + d, :], Lhl[:, ls, 0, d * 256:(d + 1) * 256], True, False))
            mms.append((o_ap, cst[:, 2 + d, :], Lhl[:, ls, 1, d * 256:(d + 1) * 256], False, True))
        pe_group(mms, ['Lb%d' % ls, 'cst'], ['ps%d' % br_])
        cv = ps[bc_][:].rearrange("p (a c) -> p a c", c=128)
        act(Xb[:, ls], ps[br_][:], AF.Exp, ['ps%d' % br_], ['Xb%d' % ls])
        act(dec[:, :, 0, t], cv[:, 0:2, 127], AF.Exp, ['ps%d' % bc_], ['dec%d' % t])
        act(dec[:, :, 1, t], cv[:, 2:4, 0], AF.Exp, ['ps%d' % bc_], ['dec%d' % t])
        if not is_ctx:
            for d in range(2):
                act(Ep[:, d, :, ti * 128:(ti + 1) * 128], cv[:, d * 2:d * 2 + 2, :], AF.Exp, ['ps%d' % bc_], ['Ep'])
                act(Em[:, d, :, ti * 128:(ti + 1) * 128], cv[:, d * 2:d * 2 + 2, :], AF.Exp, ['ps%d' % bc_], ['Em'],
                    scale=-1.0)

    def e3_dve(t):
        ls = pos_of[t] % 2
        bk = ebank[('k', t)]
        dve('tensor_tensor', ['ps%d' % bk, 'Xb%d' % ls], ['kdf%d' % ls], out=kdec_f[:, ls, :], in0=ps[bk][:, 0:256],
            in1=Xb[:, ls, 0:256], op=ALU.mult)
        dve('tensor_tensor', ['ps%d' % bk, 'Xb%d' % ls], ['kdb%d' % t], out=kdec_b[:, t, :], in0=ps[bk][:, 0:256],
            in1=Xb[:, ls, 256:512], op=ALU.mult)

    def e4(t):
        ls = pos_of[t] % 2
        kvb = kv_mm(kdec_f[:, ls, :], 'kdf%d' % ls, t)
        if t >= NT:
            scan_step(0, t, t - NT, kvb, True)
        else:
            scan_step(0, t, t, kvb, False)
            dve('tensor_tensor', ['Pn', 'dec%d' % t], ['Pn'], out=Pn[:, :, 0, t + 1], in0=Pn[:, :, 0, t],
                in1=dec[:, :, 0, t], op=ALU.mult)

    def e5(t):
        tiles, _ = blk_of(t)
        tb = tiles[0] * 128
        for mi in range(4):
            b = bank()
            slot, co = mi // 2, (mi % 2) * 128
            mms = [(ps[b][:], wch[:, slot, kc, co:co + 128], hT[:, kc, tb:tb + 512], kc == 0, kc == 7)
                   for kc in range(8)]
            pe_group(mms, hkeys(tiles) + ['w%d' % slot], ['ps%d' % b])
            hfi = mi % 2
            for d in range(2):
                if mi < 2:
                    dve('scalar_tensor_tensor', ['ps%d' % b, 'Ep'], ['qd%d' % tt for tt in tiles] + ['bc'],
                        out=qd[:, d, hfi, tb:tb + 512], in0=ps[b][:], scalar=0.125, in1=Ep[:, d, hfi, :],
                        op0=ALU.mult, op1=ALU.mult)
                else:
                    dve('tensor_tensor', ['ps%d' % b, 'Em'], ['kd%d' % tt for tt in tiles] + ['bc'],
                        out=kd[:, d, hfi, tb:tb + 512], in0=ps[b][:], in1=Em[:, d, hfi, :], op=ALU.mult)

    nseq = len(seq)
    for i in range(-2, nseq):
        if i + 2 < nseq:
            e1(seq[i + 2])
        if 0 <= i + 1 < nseq:
            t1_ = seq[i + 1]
            e3_pe(t1_)
            e2(t1_)
            e3_dve(t1_)
        if 0 <= i < nseq:
            e4(seq[i])
        if 0 <= i + 1 < nseq:
            t1_ = seq[i + 1]
            if t1_ < NT and t1_ % 4 == 3:
                e5(t1_)

    for n, t in enumerate([NT + 1, NT]):
        kvb = kv_mm(kdec_b[:, t, :], 'kdb%d' % t, t)
        scan_step(1, t, n, kvb, True)
    for n, t in enumerate(range(NT - 1, -1, -1)):
        kvb = kv_mm(kdec_b[:, t, :], 'kdb%d' % t, t)
        scan_step(1, t, n, kvb, False)
        dve('tensor_tensor', ['Pn', 'dec%d' % t], ['Pn'], out=Pn[:, :, 1, t], in0=Pn[:, :, 1, t + 1], in1=dec[:, :, 1, t],
            op=ALU.mult)

    for d in range(2):
        dve('tensor_copy', ['Sst%d0' % d], ['stg'], out=stg[:, d, :, 0:128], in_=Sst[:, d, 0, :, :])
    dve('tensor_copy', ['Pn'], ['stg'], out=stg[:, 0, :, 128], in_=Pn[:, :, 0, NT])
    dve('tensor_copy', ['Pn'], ['stg'], out=stg[:, 1, :, 128], in_=Pn[:, :, 1, 0])
    load_w(0, ZB0 + 256, 256)
    load_w(1, ZA0, 256)
    load_w(2, ZA0 + 256, 256)
    dma('pool', st_in.ap(), stg[:].rearrange("p a b c -> p (a b c)"), ['stg'], ['st_in'], 'stout')
    if not _NOCC:
        P.dma('pool', lambda e: e.collective_compute(
            "AllGather", ALU.bypass, replica_groups=[[0, 1, 2, 3], [4, 5, 6, 7]],
            ins=[st_in.ap().opt()], outs=[st_all.ap().opt()]), ['st_in'], ['st_all'], 'ccB', inc=1)

    if stop is not None:
        if end_phase(1, locals()):
            esC.close(); esX.close(); esB.close(); esA.close(); P.es.close()
            return nc
    esC.close()
    esX.close()

    esC = ExitStack()
    zb = alloc(esC, "zb", [128, 4, 2048], BF16, R)
    sv = alloc(esC, "sv", [128, 4, 2048], BF16, R)
    esC2 = ExitStack()
    stall = alloc(esC2, "stall", [128, 4, 2, 2, 132], F32, R)
    Sinit = alloc(esC2, "Sinit", [128, 2, 2, 128], F32, R)
    ftmp = alloc(esC2, "ftmp", [128, 2, 2, 128], F32, R)
    scm = alloc(esC2, "scm", [128, 2, 2, 4, 128], BF16, R)
    sqb = alloc(esC2, "sqb", [128, 2, 512], BF16, R)
    rsd = alloc(esC2, "rsd", [128, 2, 512], F32, R)
    t1 = alloc(esC2, "t1", [128, 2, 512], F32, R)


    def phase2_body():
        if _P2CUT < 1:
            return
        dma('sp', stall[:].rearrange("p r a b c -> p r (a b c)"), st_all.ap().rearrange("(r p) c -> p r c", p=128),
            ['st_all'], ['stall'], 'stin')
        for mi in range(4):
            slot, co = (3, 0)[mi // 2], (mi % 2) * 128
            for tbi in range(4):
                b = bank()
                tb = tbi * 512
                mms = [(ps[b][:], wch[:, slot, kc, co:co + 128], hT[:, kc, tb:tb + 512], kc == 0, kc == 7)
                       for kc in range(8)]
                pe_group(mms, hkeys(range(tbi * 4, tbi * 4 + 4)) + ['w%d' % slot], ['ps%d' % b])
                act(zb[:, mi, tb:tb + 512], ps[b][:], AF.Silu, ['ps%d' % b],
                    ['zb%d_%d' % (mi, tbi), 'xt0', 'xt1', 'xn0', 'xn1', 'hT16', 'hT17'])
        P.barrier(exclude=('ccB', 'stout', 'wl0', 'wl1', 'wl2', 'wl3'))

        def fproj(slot, mis, evac):
            for mi_l in range(2):
                mi = mis[mi_l]
                co = mi_l * 128
                for tbi in range(4):
                    b = bank()
                    tb = tbi * 512
                    mms = [(ps[b][:], wch[:, slot, kc, co:co + 128], hT[:, kc, tb:tb + 512], kc == 0, kc == 7)
                           for kc in range(8)]
                    pe_group(mms, hkeys(range(tbi * 4, tbi * 4 + 4)) + ['w%d' % slot], ['ps%d' % b])
                    evac(b, mi, tbi, tb)

        def ev_z(b, mi, tbi, tb):
            act(sv[:, mi, tb:tb + 512], ps[b][:], AF.Silu, ['ps%d' % b], ['sv%d_%d' % (mi, tbi)])

        def ev_u(b, mi, tbi, tb):
            key = 'sv%d_%d' % (mi, tbi)
            dve('tensor_tensor', ['ps%d' % b, key], [key], out=sv[:, mi, tb:tb + 512], in0=ps[b][:],
                in1=sv[:, mi, tb:tb + 512], op=ALU.mult)

        load_w(3, UA0, 256)
        load_w(0, UA0 + 256, 256)
        fproj(1, (0, 1), ev_z)
        fproj(2, (2, 3), ev_z)
        for d in range(2):
            for r in range(4):
                msk = fsel[:, d * 4 + r:d * 4 + r + 1]
                cmsk = fsel[:, 8 + d * 4 + r:8 + d * 4 + r + 1]
                dve('tensor_scalar_mul', ['stall', 'fsel'], ['stall'], out=stall[:, r, d, :, 0:128],
                    in0=stall[:, r, d, :, 0:128], scalar1=msk)
                dve('tensor_scalar', ['stall', 'fsel'], ['stall'], out=stall[:, r, d, :, 128:129],
                    in0=stall[:, r, d, :, 128:129], scalar1=msk, scalar2=cmsk, op0=ALU.mult, op1=ALU.add)
        fproj(3, (0, 1), ev_u)
        fproj(0, (2, 3), ev_u)

        browS = w2aug[0:1, 0:256].rearrange("p (g i) -> p g i", g=2)
        P.dma('pool', lambda e: e.dma_start(out=browS, in_=bsrows_d[0:1, 0:2, 0:128]), [], ['w2aug'], 'browl')
        for g in range(2):
            for q in range(4):
                b = bank()
                mms = []
                for i in range(4):
                    o_ap = ps[b][:, i * 128:(i + 1) * 128]
                    mms.append((o_ap, onesB[0:1, :], browS[0:1, g, :], True, False))
                    mms.append((o_ap, vn[:, q * 4 + i, g * 128:(g + 1) * 128], wsT[:, g, :], False, True))
                pe_group(mms, ['vn%d' % (q * 4 + i) for i in range(4)] + ['wsT', 'w2aug', 'onesB'], ['ps%d' % b])
                dve('tensor_tensor', ['ps%d' % b, 'sv%d_%d' % (g, q)], ['sv%d_%d' % (g, q)],
                    out=sv[:, g, q * 512:(q + 1) * 512], in0=ps[b][:], in1=sv[:, g, q * 512:(q + 1) * 512], op=ALU.mult)
        gv_ = vn_all.ap().rearrange("(r w) c -> r w c", w=64)
        allvn = ['vn%d' % tt for tt in range(NT)]
        dma('sp', vn[:, :, 256:512], gv_[:, 0:16, :], ['vn_all'], allvn, 'gathl0')
        dma('sp', vn[:, :, 0:256], gv_[:, 16:32, :], ['vn_all'], allvn, 'gathl1')

        for d in range(2):
            dve('tensor_copy', ['Sctx%d' % d], ['Sinit%d' % d], out=Sinit[:, d], in_=Sctx[:, d])
        for step in range(4):
            for d in range(2):
                r = step if d == 0 else 3 - step
                for pr in range(2):
                    dve('scalar_tensor_tensor', ['Sinit%d' % d, 'stall'], ['Sinit%d' % d], out=Sinit[:, d, pr],
                        in0=Sinit[:, d, pr], scalar=stall[:, r, d, pr, 128:129], in1=stall[:, r, d, pr, 0:128],
                        op0=ALU.mult, op1=ALU.add)
        if _P2CUT < 2:
            return
        bo_of = {}

        SPv = stall[:].rearrange("p r a b c -> p (r a b c)").bitcast(BF16)[:, 0:1024].rearrange(
            "p (s d r e) -> p s d r e", s=2, d=2, r=2)

        def fixup(t):
            sl = t % 2
            for d in range(2):
                pidx = t if d == 0 else t + 1
                dve('tensor_tensor', ['Sinit%d' % d, 'Pn'], ['SP%d' % sl, 'stall'], out=SPv[:, sl, d],
                    in0=Sinit[:, d], in1=Pn[:, :, d, pidx].unsqueeze(2).to_broadcast([128, 2, 128]), op=ALU.mult)

        def stA(t):
            s = t % 2
            tk = slice(t * 128, (t + 1) * 128)
            fixup(t)
            bx, by = bank(), bank()
            mms = []
            for d in range(2):
                for h in range(4):
                    pr, sub = h // 2, h % 2
                    bb_ = bx if sub == 0 else by
                    col = (d * 2 + pr) * 128
                    mms.append((ps[bb_][:, col:col + 128], kd[sub * 64:(sub + 1) * 64, d, pr, tk],
                                qd[sub * 64:(sub + 1) * 64, d, pr, tk], True, True))
            pe_group(mms, ['kd%d' % t, 'qd%d' % t], ['ps%d' % bx, 'ps%d' % by])
            for d in range(2):
                for sub, bb_ in ((0, bx), (1, by)):
                    dve('tensor_tensor', ['ps%d' % bb_, 'M4'], ['scm%d%d' % (s, d)], out=scm[:, s, d, sub::2, :],
                        in0=ps[bb_][:, d * 256:(d + 1) * 256].rearrange("p (h i) -> p h i", i=128),
                        in1=M4[:, d, 0:2, :], op=ALU.mult)

        def stB(t):
            s = t % 2
            tk = slice(t * 128, (t + 1) * 128)
            bo = bank()
            bo_of[t] = bo
            mms = []
            for h in range(4):
                pr, sub = h // 2, h % 2
                o_ap = ps[bo][:, h * 128:(h + 1) * 128]
                mms.append((o_ap, vtok[:, t, h * 128:(h + 1) * 128], scm[:, s, 0, h, :], True, False))
                mms.append((o_ap, vtok[:, t, h * 128:(h + 1) * 128], scm[:, s, 1, h, :], False, False))
                mms.append((o_ap, Sloc[sub * 64:(sub + 1) * 64, 0, pr, t, :], qd[sub * 64:(sub + 1) * 64, 0, pr, tk],
                            False, False))
                mms.append((o_ap, Sloc[sub * 64:(sub + 1) * 64, 1, pr, t, :], qd[sub * 64:(sub + 1) * 64, 1, pr, tk],
                            False, False))
                mms.append((o_ap, SPv[sub * 64:(sub + 1) * 64, s, 0, pr, :], qd[sub * 64:(sub + 1) * 64, 0, pr, tk],
                            False, False))
                mms.append((o_ap, SPv[sub * 64:(sub + 1) * 64, s, 1, pr, :], qd[sub * 64:(sub + 1) * 64, 1, pr, tk],
                            False, True))
            pe_group(mms, ['vtok%d' % t, 'scm%d0' % s, 'scm%d1' % s, 'Sloc0_%d' % t, 'Sloc1_%d' % t, 'qd%d' % t,
                           'SP%d' % s], ['ps%d' % bo])
            act(sqb[:, s], ps[bo][:], AF.Square, ['ps%d' % bo], ['sqb%d' % s])
            act(t1[:, s], ps[bo][:], AF.Copy, ['ps%d' % bo], ['t1%d' % s])

        def stC(t):
            s = t % 2
            tk = slice(t * 128, (t + 1) * 128)
            bo = bo_of[t]
            bs_ = bank()
            pe_group([(ps[bs_][:], onesB[:], sqb[:, s], True, True)], ['sqb%d' % s, 'onesB'], ['ps%d' % bs_])
            act(rsd[:, s], ps[bs_][:], AF.Ln, ['ps%d' % bs_], ['rsd%d' % s], bias=128.0 * EPS, scale=1.0)
            act(rsd[:, s], rsd[:, s], AF.Exp, ['rsd%d' % s], ['rsd%d' % s], scale=-0.5)
            dve('tensor_tensor', ['t1%d' % s, 'rsd%d' % s], ['t1%d' % s], out=t1[:, s], in0=t1[:, s], in1=rsd[:, s],
                op=ALU.mult)
            zkeys = ['zb%d_%d' % (h, t // 4) for h in range(4)]
            dve('tensor_tensor', ['t1%d' % s] + zkeys, zkeys, out=zb[:, :, tk],
                in0=t1[:, s].rearrange("p (h i) -> p h i", i=128), in1=zb[:, :, tk], op=ALU.mult)

        stA(0)
        for t in range(NT):
            if t + 1 < NT:
                stA(t + 1)
            stB(t)
            if t >= 1:
                stC(t - 1)
        stC(NT - 1)

    phase2_body()
    if stop is not None:
        if end_phase(2, locals()):
            esC2.close(); esC.close(); esB.close(); esA.close(); P.es.close()
            return nc
    esC2.close()
    esB.close()

    esB = ExitStack()
    gath = alloc(esB, "gath", [128, 64, 256], BF16)
    mT = alloc(esB, "mT", [128, 8, 2048], BF16)
    esC2 = ExitStack()
    wout = alloc(esC2, "wout", [128, 8, 1024], BF16, R)
    browc = wch[0:1, 3, 0:4, :].rearrange("p a b -> p (a b)").rearrange("p (g n) -> p g n", g=2)
    P.dma('pool', lambda e: e.dma_start(out=browc, in_=bsrows_d[0:1, 2:4, :]), [], ['w3'], 'browcl')
    fng = alloc(esC2, "fng", [128, 1024], F32, R)
    szb = alloc(esC2, "szb", [128, 2, 512], BF16, R)
    sg = alloc(esC2, "sg", [128, 2, 2, 512], F32, R)
    ost = alloc(esC2, "ost", [128, 16], F32, R)

    def colmix(wq):
        for g in range(2):
            svr = sv[:, 2 + g, :].rearrange("p (r w) -> p r w", w=64)
            b = bank()
            pcol = ps[b][:].rearrange("p (r w) -> p w r", w=16)
            mms = [(ps[b][:], onesB[0:1, :], browc[0:1, g, :], True, False)]
            if wq == 0:
                src = lambda wi: vn[:, wi, 256 + g * 128:256 + (g + 1) * 128]
            elif wq == 1:
                src = lambda wi: vn[:, wi, g * 128:(g + 1) * 128]
            else:
                src = lambda wi: gath[:, wq * 16 + wi, g * 128:(g + 1) * 128]
            mms += [(pcol[:, wi, :], src(wi), wsTcol[:, g, :], False, wi == 15) for wi in range(16)]
            gkeys = ['vn%d' % tt for tt in range(NT)] if wq < 2 else ['gath%d' % wq]
            pe_group(mms, gkeys + ['wsTcol', 'w3', 'onesB'], ['ps%d' % b])
            keys = ['sv%d_%d' % (2 + g, q) for q in range(4)]
            dve('tensor_tensor', ['ps%d' % b] + keys, keys, out=svr[:, :, wq * 16:(wq + 1) * 16],
                in0=ps[b][:].rearrange("p (r w) -> p r w", w=16), in1=svr[:, :, wq * 16:(wq + 1) * 16], op=ALU.mult)


    colmix(0)
    colmix(1)
    gv = vn_all.ap().rearrange("(r w) c -> r w c", w=64)
    for wq in range(2, 4):
        dma('sp', gath[:, wq * 16:(wq + 1) * 16, :], gv[:, wq * 16:(wq + 1) * 16, :], ['vn_all'], ['gath%d' % wq] + ['kd%d' % tt for tt in range(NT)] + ['qd%d' % tt for tt in range(NT)],
            'gathl%d' % wq)

    P.barrier()
    P.dma_group('sp', [(lambda e: e.dma_start(out=fng[:], in_=fng_d[:, :]), [], ['fng'])], 'setup')
    dve('memset', [], ['ost'], ap=ost[:], constant=0.0)
    colmix(2)
    colmix(3)

    wpab = gath[:].rearrange("p w c -> p (w c)")
    wpa_s = wpab[:, 0:4096].rearrange("p (f n) -> p f n", n=1024)
    wpb_s = wpab[:, 4096:8192].rearrange("p (f n) -> p f n", n=1024)
    P.dma('pool', lambda e: e.dma_start(out=wpa_s, in_=wpa_d.rearrange("(f p) n -> p f n", p=128)),
          [], ['gathA'], 'wpal')
    P.dma('pool', lambda e: e.dma_start(out=wpb_s, in_=wpb_d.rearrange("(f p) n -> p f n", p=128)),
          [], ['gathA'], 'wpbl')
    for h in range(4):
        act(wpb_s[:, h, :], wpb_s[:, h, :], AF.Copy, ['gathA', 'gs'], ['gathA'], scale=gs[:, h:h + 1])
    for i in range(8):
        slot = i % 2
        P.dma('pool', lambda e, i=i, slot=slot: e.dma_start(
            out=wch[:, slot, :, 0:128], in_=win_v[:, :, G0 + i * 128:G0 + (i + 1) * 128]), [], ['w%d' % slot],
            'wl%d' % slot)
        P.dma('pool', lambda e, i=i, slot=slot: e.dma_start(
            out=wch[:, slot, :, 128:256], in_=win_v[:, :, G0 + 1024 + i * 128:G0 + 1024 + (i + 1) * 128]), [],
            ['w%d' % slot], 'wl%d' % slot)
        for tbi in range(4):
            tb = tbi * 512
            hk = hkeys(range(tbi * 4, tbi * 4 + 4))
            s = tbi % 2
            bg = []
            for gi in range(2):
                b = bank()
                mms = [(ps[b][:], wch[:, slot, kc, gi * 128:(gi + 1) * 128], hT[:, kc, tb:tb + 512], kc == 0, kc == 7)
                       for kc in range(8)]
                pe_group(mms, hk + ['w%d' % slot], ['ps%d' % b])
                act(sg[:, s, gi], ps[b][:], AF.Sigmoid, ['ps%d' % b], ['sg%d%d' % (s, gi)])
                bg.append(b)
            ba = bank()
            mms = [(ps[ba][:], wpa_s[:, fc, i * 128:(i + 1) * 128], sv[:, fc, tb:tb + 512], fc == 0, fc == 3)
                   for fc in range(4)]
            pe_group(mms, ['gathA'] + ['sv%d_%d' % (fc, tbi) for fc in range(4)], ['ps%d' % ba])
            bb = bank()
            mms = [(ps[bb][:], wpb_s[:, fc, i * 128:(i + 1) * 128], zb[:, fc, tb:tb + 512], fc == 0, fc == 3)
                   for fc in range(4)]
            pe_group(mms, ['gathA'] + ['zb%d_%d' % (fc, tbi) for fc in range(4)], ['ps%d' % bb])
            dve('tensor_tensor', ['ps%d' % ba, 'sg%d0' % s], ['sg%d0' % s], out=sg[:, s, 0], in0=ps[ba][:],
                in1=sg[:, s, 0], op=ALU.mult)
            dve('tensor_tensor', ['ps%d' % bb, 'sg%d1' % s], ['sg%d1' % s], out=sg[:, s, 1], in0=ps[bb][:],
                in1=sg[:, s, 1], op=ALU.mult)
            dve('tensor_tensor', ['sg%d0' % s, 'sg%d1' % s], ['mT%d' % tbi], out=mT[:, i, tb:tb + 512],
                in0=sg[:, s, 0], in1=sg[:, s, 1], op=ALU.add)
        if i == 1:
            P.dma('pool', lambda e: e.dma_start(out=wout[:], in_=wout_d.rearrange("(kc p) n -> p kc n", p=128)),
                  [], ['wout'], 'woutl')
        if i == 4:
            for kc in range(8):
                dve('tensor_tensor', ['wout', 'gate_bc'], ['wout'], out=wout[:, kc, :], in0=wout[:, kc, :],
                    in1=gate_bc[:], op=ALU.mult)

    hTf = hT[:].rearrange("p k t -> p (k t)").bitcast(F32)
    for t in range(NT):
        s = t % 2
        xr = hTf[:, s * 1024:(s + 1) * 1024]
        xo = hTf[:, 2048 + s * 1024:2048 + (s + 1) * 1024]
        allh = hkeys(range(NT)) if t < 2 else []
        dma('sp', xr, x_d[t * 128:(t + 1) * 128, :], [], allh + ['xr%d' % s], 'xr%d' % s)
        for hf_ in range(2):
            b = bank()
            mms = [(ps[b][:], mT[:, kc, t * 128:(t + 1) * 128], wout[:, kc, hf_ * 512:(hf_ + 1) * 512], kc == 0, kc == 7)
                   for kc in range(8)]
            pe_group(mms, ['mT%d' % (t // 4), 'wout'], ['ps%d' % b])
            dve('tensor_tensor', ['ps%d' % b, 'xr%d' % s] + allh, ['xo%d' % s] + allh,
                out=xo[:, hf_ * 512:(hf_ + 1) * 512], in0=ps[b][:], in1=xr[:, hf_ * 512:(hf_ + 1) * 512],
                op=ALU.add)
        act(xr, xo, AF.Square, ['xo%d' % s], ['xr%d' % s, 'ost'], accum_out=ost[:, t:t + 1])
        rstd_from(ost[:, t:t + 1], ost[:, t:t + 1], 1.0 / 1024, EPS, 'ost', 'ost')
        dve('scalar_tensor_tensor', ['xo%d' % s, 'ost', 'fng'], ['xo%d' % s], out=xo, in0=xo, scalar=ost[:, t:t + 1],
            in1=fng[:], op0=ALU.mult, op1=ALU.mult)
        dma('pool', out_d[t * 128:(t + 1) * 128, :], xo, ['xo%d' % s], ['out'], 'ost%d' % s)

    end_phase(3, locals())
    esC2.close()
    esB.close()
    esC.close()
    esA.close()
    P.es.close()
    return nc


_NC_CACHE = {}


def _prep(inp):
    f = lambda a: np.ascontiguousarray(np.asarray(a, dtype=np.float32))
    x = f(inp['x']); c = f(inp['c']); ctx = f(inp['ctx']); c_ctx = f(inp['c_ctx'])
    w_mod = f(inp['w_mod'])[0]; b_mod = f(inp['b_mod'])[0]; norm_g = f(inp['norm_g'])[0]
    w_in = f(inp['w_in'])[0]; a_ln_g = f(inp['a_ln_g'])[0]; a_ln_b = f(inp['a_ln_b'])[0]
    a_ws = f(inp['a_ws'])[0]; a_bs = f(inp['a_bs'])[0]
    w2 = f(inp['b_gate_w2'])[0]; gb = f(inp['b_gate_b'])[0]; b_norm_g = f(inp['b_norm_g'])[0]
    wpa = f(inp['w_proj_a'])[0]; wpb = f(inp['w_proj_b'])[0]; wout = f(inp['w_out'])[0]
    fng = f(inp['final_norm_g'])

    def colT(v):
        return np.ascontiguousarray(v.reshape(-1, 128).T)

    j = np.arange(128)[:, None]; i = np.arange(128)[None, :]
    s = np.float32(-1.0 / 16.0)
    z = np.float32(0)
    consts = np.stack([
        np.eye(128, dtype=np.float32),
        np.where(j <= i, s, z), np.where(j >= i, s, z),
        np.where(j > i, s, z), np.where(j < i, s, z),
        (j <= i).astype(np.float32), (j >= i).astype(np.float32)]).astype(np.float32)
    w2aug = np.zeros((33, 512), np.float32)
    w2aug[0:16, 0:256] = w2[0]
    w2aug[16:32, 256:512] = w2[1]
    w2aug[32, 0:256] = gb[0]
    w2aug[32, 256:512] = gb[1]
    wsT = np.ascontiguousarray(np.transpose(a_ws, (0, 2, 1)))
    shared = {
        "cctxT": colT(c_ctx), "wmod": w_mod, "bmod": np.ascontiguousarray(b_mod[None, :]),
        "normgT": colT(norm_g), "win": w_in,
        "alng": np.ascontiguousarray(np.broadcast_to(a_ln_g[None, :], (128, 512))),
        "alnb": np.ascontiguousarray(np.broadcast_to(a_ln_b[None, :], (128, 512))),
        "wsT": np.ascontiguousarray(wsT[0:2]),
        "bsrow": np.ascontiguousarray(np.broadcast_to(a_bs[None, 0:2, :], (128, 2, 128))),
        "w2aug": w2aug, "bnormgT": np.ascontiguousarray(b_norm_g.reshape(4, 128).T),
        "wpa": wpa, "wpb": wpb, "wout": wout,
        "fng": np.ascontiguousarray(np.broadcast_to(fng[None, :], (128, 1024))),
        "consts": consts,
        "normgbc": np.ascontiguousarray(np.broadcast_to(norm_g[None, :], (128, 1024))),
    }
    in_maps = []
    for core in range(8):
        b, jj = core // 4, core % 4
        m = dict(shared)
        m["x"] = np.ascontiguousarray(x[b, 2048 * jj:2048 * (jj + 1), :])
        m["ctx"] = np.ascontiguousarray(ctx[b])
        m["cT"] = colT(c[b])
        m["wsTcol"] = np.ascontiguousarray(wsT[2:4, :, 32 * jj:32 * (jj + 1)])
        m["bscol"] = np.ascontiguousarray(np.broadcast_to(a_bs[None, 2:4, 32 * jj:32 * (jj + 1)], (128, 2, 32)))
        fs = np.zeros((128, 16), np.float32)
        for r in range(4):
            fs[:, r] = 1.0 if r < jj else 0.0
            fs[:, 4 + r] = 1.0 if r > jj else 0.0
            fs[:, 8 + r] = 0.0 if r < jj else 1.0
            fs[:, 12 + r] = 0.0 if r > jj else 1.0
        m["fsel"] = fs
        br = np.zeros((1, 4, 512), np.float32)
        for g in range(2):
            br[0, g] = np.tile(a_bs[g], 4)
            br[0, 2 + g] = np.repeat(a_bs[2 + g, 32 * jj:32 * (jj + 1)], 16)
        m["bsrows"] = br
        in_maps.append(m)
    return in_maps


def kernel(**inputs):
    in_maps = _prep(inputs)
    if 'nc' not in _NC_CACHE:
        _NC_CACHE['nc'] = build_nc()
    nc = _NC_CACHE['nc']
    res = run_bass_kernel_spmd(nc, in_maps, core_ids=list(range(8)))
    out = np.zeros((2, 8192, 1024), np.float32)
    for core in range(8):
        b, jj = core // 4, core % 4
        out[b, 2048 * jj:2048 * (jj + 1), :] = np.asarray(res.results[core]["out"], np.float32)
    return out
```
